# Optimizing a Trainium2 kernel written in Bass

```python
import jax, jax.numpy as jnp
from jax import lax
import numpy as np

D_MODEL = 1024
BATCH = 16
SEQ = 2048
DEPTH = 2

GRID_W = 64
CTX_LEN = 256
EPS = 1e-6

A_GROUPS = 4
A_GROUP_DIM = 128
D_A = A_GROUPS * A_GROUP_DIM
D_B = 512
CONV_WIDTH = 31
EVEN_IN = D_A + 2 * D_B
EVEN_OUT = D_A + D_B

HEAD_DIM = 64
N_Q_HEADS = 12
N_KV_HEADS = 4
Q_PER_KV = N_Q_HEADS // N_KV_HEADS
D_Q = N_Q_HEADS * HEAD_DIM
D_KV = N_KV_HEADS * HEAD_DIM
POOL_WINDOWS = (2, 4, 8, 16)
POOL_GROUPS = len(POOL_WINDOWS)
POOL_GROUP_DIM = 64
D_POOL = POOL_GROUPS * POOL_GROUP_DIM
ODD_IN = D_Q + 2 * D_KV + D_POOL
ODD_OUT = D_Q + D_POOL
Q_BLOCK = 128
ROPE_THETA = 10000.0
ROPE_PAIRS = HEAD_DIM // 4

D_FF = 2816
FFN_CONV_WIDTH = 3

kernel_name = "hybrid_fourier_conformer_gqa_pool_dit"


def rmsnorm(x, g):
    xf = x.astype(jnp.float32)
    y = xf * lax.rsqrt(jnp.mean(xf * xf, axis=-1, keepdims=True) + EPS)
    return (y * g.astype(jnp.float32)).astype(x.dtype)


def layernorm(x, g, b):
    xf = x.astype(jnp.float32)
    mu = jnp.mean(xf, axis=-1, keepdims=True)
    var = jnp.mean(jnp.square(xf - mu), axis=-1, keepdims=True)
    y = (xf - mu) * lax.rsqrt(var + EPS)
    return (y * g.astype(jnp.float32) + b.astype(jnp.float32)).astype(x.dtype)


def modulate(h, shift, scale):
    return h * (1 + scale) + shift


def dwconv(x, w):
    C = x.shape[-1]
    return lax.conv_general_dilated(
        x, w[:, None, :].astype(x.dtype), window_strides=(1,), padding='SAME',
        dimension_numbers=('NWC', 'WIO', 'NWC'), feature_group_count=C)


def rope_axis(x, cos, sin):
    x1, x2 = jnp.split(x, 2, axis=-1)
    return jnp.concatenate([x1 * cos - x2 * sin, x2 * cos + x1 * sin], axis=-1)


def apply_axial_rope(x, rope):
    cr, sr, cc, sc = [t.astype(x.dtype) for t in rope]
    half = HEAD_DIM // 2
    return jnp.concatenate([rope_axis(x[..., :half], cr, sr),
                            rope_axis(x[..., half:], cc, sc)], axis=-1)


def even_mixer(h, w_in, conv_w, ln_g, ln_b, w_out):
    Bn, L, _ = h.shape
    p = h @ w_in
    a = p[..., :D_A].reshape(Bn, L, A_GROUPS, A_GROUP_DIM)
    fa = jnp.fft.fft2(a.astype(jnp.float32), axes=(1, 3), norm='ortho').real
    fa = fa.astype(h.dtype).reshape(Bn, L, D_A)
    u = p[..., D_A:D_A + D_B]
    g = p[..., D_A + D_B:]
    b = u * jax.nn.sigmoid(g)
    b = jax.nn.silu(layernorm(dwconv(b, conv_w), ln_g, ln_b))
    return jnp.concatenate([fa, b], axis=-1) @ w_out


def gqa_softmax(q, k, v):
    Bn, Lq = q.shape[0], q.shape[1]
    qg = q.reshape(Bn, Lq, N_KV_HEADS, Q_PER_KV, HEAD_DIM).astype(jnp.float32)
    s = jnp.einsum('bqhgd,bkhd->bhgqk', qg, k.astype(jnp.float32)) * (HEAD_DIM ** -0.5)
    pr = jax.nn.softmax(s, axis=-1)
    o = jnp.einsum('bhgqk,bkhd->bqhgd', pr.astype(v.dtype), v)
    return o.reshape(Bn, Lq, D_Q)


def blocked_attention(q, k_all, v_all):
    Bn, S = q.shape[0], q.shape[1]
    nb = S // Q_BLOCK
    qb = q.reshape(Bn, nb, Q_BLOCK, N_Q_HEADS, HEAD_DIM).swapaxes(0, 1)
    o = lax.map(lambda qblk: gqa_softmax(qblk, k_all, v_all), qb)
    return o.swapaxes(0, 1).reshape(Bn, S, D_Q)


def multiscale_pool(u, w_pool, scale):
    Bn, L, _ = u.shape
    uf = u.astype(jnp.float32)
    cs = jnp.concatenate([jnp.zeros((Bn, 1, D_POOL), jnp.float32), jnp.cumsum(uf, axis=1)], axis=1)
    t = jnp.arange(L)
    outs = []
    for gi, w in enumerate(POOL_WINDOWS):
        sl = slice(gi * POOL_GROUP_DIM, (gi + 1) * POOL_GROUP_DIM)
        lo = jnp.maximum(t - w // 2, 0)
        hi = jnp.minimum(t + w // 2 - 1, L - 1) + 1
        cnt = (hi - lo).astype(jnp.float32)[:, None]
        csg = cs[..., sl]
        mean = (jnp.take(csg, hi, axis=1) - jnp.take(csg, lo, axis=1)) / cnt
        outs.append(mean - uf[..., sl])
    pooled = jnp.stack(outs, axis=2)
    mixed = jnp.einsum('blgc,gcd->blgd', pooled, w_pool.astype(jnp.float32)).reshape(Bn, L, D_POOL)
    return (mixed * scale.astype(jnp.float32)).astype(u.dtype)


def odd_mixer(h, hc, w_in, q_g, k_g, pool_w, pool_scale, w_out, rope, need_ctx):
    Bn, L, _ = h.shape
    p = h @ w_in
    q = p[..., :D_Q].reshape(Bn, L, N_Q_HEADS, HEAD_DIM)
    k = p[..., D_Q:D_Q + D_KV].reshape(Bn, L, N_KV_HEADS, HEAD_DIM)
    v = p[..., D_Q + D_KV:D_Q + 2 * D_KV].reshape(Bn, L, N_KV_HEADS, HEAD_DIM)
    u = p[..., D_Q + 2 * D_KV:]
    q = apply_axial_rope(rmsnorm(q, q_g), rope)
    k = apply_axial_rope(rmsnorm(k, k_g), rope)
    Lc = hc.shape[1]
    if need_ctx:
        pc = hc @ w_in
        kv_c = pc[..., D_Q:D_Q + 2 * D_KV]
    else:
        kv_c = hc @ w_in[:, D_Q:D_Q + 2 * D_KV]
    kc = rmsnorm(kv_c[..., :D_KV].reshape(Bn, Lc, N_KV_HEADS, HEAD_DIM), k_g)
    vc = kv_c[..., D_KV:].reshape(Bn, Lc, N_KV_HEADS, HEAD_DIM)
    k_all = jnp.concatenate([kc, k], axis=1)
    v_all = jnp.concatenate([vc, v], axis=1)
    attn = blocked_attention(q, k_all, v_all)
    y = jnp.concatenate([attn, multiscale_pool(u, pool_w, pool_scale)], axis=-1) @ w_out
    yc = None
    if need_ctx:
        qc = rmsnorm(pc[..., :D_Q].reshape(Bn, Lc, N_Q_HEADS, HEAD_DIM), q_g)
        attn_c = gqa_softmax(qc, kc, vc)
        pool_c = multiscale_pool(pc[..., D_Q + 2 * D_KV:], pool_w, pool_scale)
        yc = jnp.concatenate([attn_c, pool_c], axis=-1) @ w_out
    return y, yc


def conv_ffn(h, w_up, w_conv, w_down):
    u = dwconv(h @ w_up, w_conv)
    g, val = u[..., :D_FF], u[..., D_FF:]
    return (jax.nn.silu(g) * val) @ w_down


def setup_inputs(seed: int = 0) -> dict:
    key = jax.random.key(seed)
    ks = jax.random.split(key, 24)
    n_even = (DEPTH + 1) // 2
    n_odd = DEPTH // 2
    f32 = jnp.float32

    def nrm(k, shape, scale):
        return jax.random.normal(k, shape, f32) * scale

    def gain(k, shape):
        return 1.0 + 0.1 * jax.random.normal(k, shape, f32)

    return {
        'x': nrm(ks[0], (BATCH, SEQ, D_MODEL), 1.0),
        'c': nrm(ks[1], (BATCH, D_MODEL), 1.0),
        'ctx': nrm(ks[2], (BATCH, CTX_LEN, D_MODEL), 1.0),
        'c_ctx': nrm(ks[3], (D_MODEL,), 1.0),
        'w_ada': nrm(ks[4], (DEPTH, D_MODEL, 6 * D_MODEL), D_MODEL ** -0.5),
        'b_ada': nrm(ks[5], (DEPTH, 6 * D_MODEL), 0.02),
        'norm1_g': gain(ks[6], (DEPTH, D_MODEL)),
        'norm2_g': gain(ks[7], (DEPTH, D_MODEL)),
        'ev_w_in': nrm(ks[8], (n_even, D_MODEL, EVEN_IN), D_MODEL ** -0.5),
        'ev_conv_w': nrm(ks[9], (n_even, CONV_WIDTH, D_B), CONV_WIDTH ** -0.5),
        'ev_ln_g': gain(ks[10], (n_even, D_B)),
        'ev_ln_b': nrm(ks[11], (n_even, D_B), 0.02),
        'ev_w_out': nrm(ks[12], (n_even, EVEN_OUT, D_MODEL), EVEN_OUT ** -0.5),
        'od_w_in': nrm(ks[13], (n_odd, D_MODEL, ODD_IN), D_MODEL ** -0.5),
        'od_q_g': gain(ks[14], (n_odd, HEAD_DIM)),
        'od_k_g': gain(ks[15], (n_odd, HEAD_DIM)),
        'od_pool_w': nrm(ks[16], (n_odd, POOL_GROUPS, POOL_GROUP_DIM, POOL_GROUP_DIM), POOL_GROUP_DIM ** -0.5),
        'od_pool_scale': gain(ks[17], (n_odd, D_POOL)),
        'od_w_out': nrm(ks[18], (n_odd, ODD_OUT, D_MODEL), ODD_OUT ** -0.5),
        'ffn_w_up': nrm(ks[19], (DEPTH, D_MODEL, 2 * D_FF), D_MODEL ** -0.5),
        'ffn_conv_w': nrm(ks[20], (DEPTH, FFN_CONV_WIDTH, 2 * D_FF), FFN_CONV_WIDTH ** -0.5),
        'ffn_w_down': nrm(ks[21], (DEPTH, D_FF, D_MODEL), D_FF ** -0.5),
    }


def reference(x, c, ctx, c_ctx, w_ada, b_ada, norm1_g, norm2_g,
              ev_w_in, ev_conv_w, ev_ln_g, ev_ln_b, ev_w_out,
              od_w_in, od_q_g, od_k_g, od_pool_w, od_pool_scale, od_w_out,
              ffn_w_up, ffn_conv_w, ffn_w_down):
    S = x.shape[1]
    rows = S // GRID_W
    row_ids = jnp.repeat(jnp.arange(rows), GRID_W).astype(jnp.float32)
    col_ids = jnp.tile(jnp.arange(GRID_W), rows).astype(jnp.float32)
    freqs = ROPE_THETA ** (-jnp.arange(ROPE_PAIRS, dtype=jnp.float32) / ROPE_PAIRS)
    ang_r = (row_ids[:, None] * freqs)[:, None, :]
    ang_c = (col_ids[:, None] * freqs)[:, None, :]
    rope = (jnp.cos(ang_r), jnp.sin(ang_r), jnp.cos(ang_c), jnp.sin(ang_c))

    for i in range(DEPTH):
        last = i == DEPTH - 1
        j = i // 2
        mod = (jax.nn.silu(c) @ w_ada[i] + b_ada[i])[:, None, :]
        sh1, sc1, g1, sh2, sc2, g2 = jnp.split(mod, 6, axis=-1)
        modc = jax.nn.silu(c_ctx) @ w_ada[i] + b_ada[i]
        shc1, scc1, gc1, shc2, scc2, gc2 = jnp.split(modc, 6, axis=-1)

        h = modulate(rmsnorm(x, norm1_g[i]), sh1, sc1)
        hc = modulate(rmsnorm(ctx, norm1_g[i]), shc1, scc1)
        if i % 2 == 0:
            y = even_mixer(h, ev_w_in[j], ev_conv_w[j], ev_ln_g[j], ev_ln_b[j], ev_w_out[j])
            yc = None if last else even_mixer(hc, ev_w_in[j], ev_conv_w[j], ev_ln_g[j], ev_ln_b[j], ev_w_out[j])
        else:
            y, yc = odd_mixer(h, hc, od_w_in[j], od_q_g[j], od_k_g[j], od_pool_w[j],
                              od_pool_scale[j], od_w_out[j], rope, not last)
        x = x + g1 * y
        x = x + g2 * conv_ffn(modulate(rmsnorm(x, norm2_g[i]), sh2, sc2),
                              ffn_w_up[i], ffn_conv_w[i], ffn_w_down[i])
        if not last:
            ctx = ctx + gc1 * yc
            ctx = ctx + gc2 * conv_ffn(modulate(rmsnorm(ctx, norm2_g[i]), shc2, scc2),
                                       ffn_w_up[i], ffn_conv_w[i], ffn_w_down[i])
    return x
```

```python
import math
from contextlib import ExitStack

import numpy as np
import ml_dtypes

import concourse.bass as bass
import concourse.mybir as mybir
from concourse.bass_utils import run_bass_kernel_spmd

F32 = mybir.dt.float32
BF16 = mybir.dt.bfloat16
AF = mybir.ActivationFunctionType
ALU = mybir.AluOpType

D = 1024
S = 2048
LC = 256
NCORES = 8
BPC = 2
DFF = 2816
NPAIR = 22
EPS = 1e-6
GROUPS = [(0, 8), (8, 15), (15, 22)]


class Buf:
    def __init__(self, name, handle, shape, space="sb", base=0, esize=4):
        self.name, self.space, self.base, self.esize = name, space, base, esize
        self.h = handle
        self.shape = list(shape)
        st = [1] * len(shape)
        for i in range(len(shape) - 2, 0, -1):
            st[i] = st[i + 1] * shape[i + 1]
        self.strides = st

    def __getitem__(self, idx):
        if not isinstance(idx, tuple):
            idx = (idx,)
        idx = tuple(idx) + (slice(None),) * (len(self.shape) - len(idx))
        lo = hi = 0
        for d in range(1, len(self.shape)):
            i = idx[d]
            if isinstance(i, int):
                a, b = i, i
            else:
                a, b, s = i.indices(self.shape[d])
                n = max(0, (b - a + s - 1) // s)
                b = a + (n - 1) * s
            lo += a * self.strides[d]
            hi += b * self.strides[d]
        return View(self, self.h[idx], lo, hi + 1)

    def all(self):
        return self[tuple(slice(None) for _ in self.shape)]


class View:
    __slots__ = ("buf", "ap", "lo", "hi")

    def __init__(self, buf, ap, lo, hi):
        self.buf, self.ap, self.lo, self.hi = buf, ap, lo, hi

    @property
    def res(self):
        b = self.buf
        return (b.space, b.base + self.lo * b.esize, b.base + self.hi * b.esize)


class Op:
    __slots__ = ("eng", "fn", "deps", "signal", "dma", "semkey", "cnt", "waits")


class Prog:
    ENGS = ("pe", "act", "dve", "pool", "sp")

    def __init__(self, nc):
        self.nc = nc
        self.ops = []
        self.hist = {}

    def add(self, eng, fn, reads=(), writes=(), dma=False, semkey=None):
        idx = len(self.ops)
        reads = [r.res for r in reads if r is not None]
        writes = [w.res for w in writes if w is not None]
        deps = set()
        hist = self.hist
        for (name, lo, hi) in reads:
            for r in hist.get(name, ()):
                if r[3] and r[0] < hi and lo < r[1]:
                    deps.add(r[2])
        for (name, lo, hi) in writes:
            for r in hist.get(name, ()):
                if r[0] < hi and lo < r[1]:
                    deps.add(r[2])
        for (name, lo, hi) in writes:
            lst = hist.setdefault(name, [])
            lst[:] = [r for r in lst if not (lo <= r[0] and r[1] <= hi)]
            lst.append((lo, hi, idx, True, eng, dma))
        for (name, lo, hi) in reads:
            lst = hist.setdefault(name, [])
            if not dma:
                lst[:] = [r for r in lst if not ((not r[3]) and r[4] == eng and (not r[5])
                                                 and lo <= r[0] and r[1] <= hi)]
            lst.append((lo, hi, idx, False, eng, dma))
        op = Op()
        op.eng, op.fn, op.dma, op.semkey = eng, fn, dma, semkey
        op.deps, op.signal, op.cnt, op.waits = deps, dma, 0, None
        self.ops.append(op)
        return idx

    def finalize(self, stack):
        nc, ops = self.nc, self.ops

        def skip(p, op):
            return (not p.dma) and (not op.dma) and p.eng == "pe" and op.eng == "pe"

        for op in ops:
            for d in op.deps:
                p = ops[d]
                if not p.dma and not skip(p, op):
                    p.signal = True
        sems = {}

        def sem(key):
            if key not in sems:
                sems[key] = stack.enter_context(nc.semaphore("s%d" % len(sems)))
            return sems[key]

        cnt = {}
        for op in ops:
            key = ("d", op.semkey) if op.dma else ("e", op.eng)
            if op.signal:
                cnt[key] = cnt.get(key, 0) + (16 if op.dma else 1)
                op.cnt = cnt[key]
        known = {e: {} for e in self.ENGS}
        for op in ops:
            w = {}
            for d in op.deps:
                p = ops[d]
                if skip(p, op):
                    continue
                key = ("d", p.semkey) if p.dma else ("e", p.eng)
                if p.cnt > w.get(key, 0):
                    w[key] = p.cnt
            kn = known[op.eng]
            out = []
            for key, v in w.items():
                if kn.get(key, 0) >= v:
                    continue
                kn[key] = v
                out.append((sem(key), v))
            op.waits = out
            if op.signal:
                sem(("d", op.semkey) if op.dma else ("e", op.eng))
        self.sems = sems

    def emit(self):
        nc = self.nc
        by = {e: [op for op in self.ops if op.eng == e] for e in self.ENGS}
        sems = self.sems

        def run(e, lst):
            for op in lst:
                for (s, v) in op.waits:
                    e.wait_ge(s, v)
                ins = op.fn(e)
                if op.signal:
                    key = ("d", op.semkey) if op.dma else ("e", op.eng)
                    ins.then_inc(sems[key], 16 if op.dma else 1)

        with nc.Block() as block:
            @block.tensor
            def _(e):
                run(e, by["pe"])

            @block.scalar
            def _(e):
                run(e, by["act"])

            @block.vector
            def _(e):
                run(e, by["dve"])

            @block.gpsimd
            def _(e):
                run(e, by["pool"])

            @block.sync
            def _(e):
                run(e, by["sp"])
                fin = {}
                for op in self.ops:
                    if op.dma:
                        fin[("d", op.semkey)] = op.cnt
                for key, v in fin.items():
                    e.wait_ge(sems[key], v)


SB_LO = 16384
SB_HI = 229376 - 256

OFF_X = SB_LO
OFF_C = OFF_X + 8 * S * 4
OFF_HTX = OFF_C + 8 * LC * 4
OFF_HTC = OFF_HTX + 8 * S * 2
OFF_WB = OFF_HTC + 8 * LC * 2
NWB = 3
WB_BYTES = 8192
OFF_CONST = OFF_WB + NWB * WB_BYTES
CONST_BYTES = 8704
OFF_SCR = OFF_CONST + CONST_BYTES
SCR_BYTES = SB_HI - OFF_SCR


class Seq:
    pass


class _Stop(Exception):
    pass


class Builder:
    def __init__(self, stop_after=None):
        self.stop_after = stop_after
        self.nc = bass.Bass("TRN2", target_bir_lowering=False)
        self.P = Prog(self.nc)
        self.uid = 0
        self.ps_ptr = 0
        self.wb_ptr = 0
        self.const_ptr = 0
        self.wq = []
        self.wq_issued = 0
        self.wq_next = 0
        self.wq_bufs = {}

    def ck(self, name):
        if self.stop_after == name:
            raise _Stop()

    def sb(self, shape, dt, off, name=None):
        self.uid += 1
        name = "%s_%d" % (name or "t", self.uid)
        es = 4 if dt == F32 else 2
        n = 1
        for s_ in shape[1:]:
            n *= s_
        assert off >= SB_LO and off + n * es <= SB_HI, (name, off, n * es)
        h = self.nc.alloc_sbuf_tensor_at(name, list(shape), dt, offset=off)
        return Buf(name, h, shape, "sb", off, es)

    def scr(self, shape, dt, off, name=None):
        es = 4 if dt == F32 else 2
        n = 1
        for s_ in shape[1:]:
            n *= s_
        assert off + n * es <= SCR_BYTES, (name, off, n * es, SCR_BYTES)
        return self.sb(shape, dt, OFF_SCR + off, name)

    def const(self, shape, dt, name=None):
        es = 4 if dt == F32 else 2
        n = 1
        for s_ in shape[1:]:
            n *= s_
        off = (self.const_ptr + 31) // 32 * 32
        assert off + n * es <= CONST_BYTES, ("const overflow", name)
        self.const_ptr = off + n * es
        return self.sb(shape, dt, OFF_CONST + off, name)

    def psum(self, ncols):
        nb = (ncols + 511) // 512
        nb = {1: 1, 2: 2, 3: 4, 4: 4}[nb]
        p = (self.ps_ptr + nb - 1) // nb * nb
        if p + nb > 8:
            p = 0
        self.ps_ptr = (p + nb) % 8
        return p * 512

    def mm(self, out, lhsT, rhs, start, stop):
        self.P.add("pe", lambda e: e.matmul(out.ap, lhsT=lhsT.ap, rhs=rhs.ap, start=start, stop=stop),
                   reads=[lhsT, rhs], writes=[out])

    def act(self, out, in_, func, scale=1.0, bias=None, eng="act"):
        rd = [in_]
        kw = {}
        if isinstance(scale, View):
            rd.append(scale)
            kw["scale"] = scale.ap
        else:
            kw["scale"] = float(scale)
        if isinstance(bias, View):
            rd.append(bias)
            kw["bias"] = bias.ap
        elif bias is not None:
            kw["bias"] = float(bias)
        self.P.add("act", lambda e: e.activation(out=out.ap, in_=in_.ap, func=func, **kw),
                   reads=rd, writes=[out])

    def tt(self, eng, out, in0, in1, op):
        self.P.add(eng, lambda e: e.tensor_tensor(out=out.ap, in0=in0.ap, in1=in1.ap, op=op),
                   reads=[in0, in1], writes=[out])

    def ts(self, eng, out, in0, s1, op0, s2=None, op1=None):
        rd = [in0]
        a1 = s1.ap if isinstance(s1, View) else float(s1)
        if isinstance(s1, View):
            rd.append(s1)
        a2 = None
        if s2 is not None:
            a2 = s2.ap if isinstance(s2, View) else float(s2)
            if isinstance(s2, View):
                rd.append(s2)
        if op1 is None:
            self.P.add(eng, lambda e: e.tensor_scalar(out=out.ap, in0=in0.ap, scalar1=a1, scalar2=None, op0=op0),
                       reads=rd, writes=[out])
        else:
            self.P.add(eng, lambda e: e.tensor_scalar(out=out.ap, in0=in0.ap, scalar1=a1, scalar2=a2, op0=op0, op1=op1),
                       reads=rd, writes=[out])

    def stt(self, eng, out, in0, scalar, in1, op0, op1):
        rd = [in0, in1]
        a = scalar.ap if isinstance(scalar, View) else float(scalar)
        if isinstance(scalar, View):
            rd.append(scalar)
        self.P.add(eng, lambda e: e.scalar_tensor_tensor(out=out.ap, in0=in0.ap, scalar=a, in1=in1.ap, op0=op0, op1=op1),
                   reads=rd, writes=[out])

    def copy(self, eng, out, in_):
        self.P.add(eng, lambda e: e.tensor_copy(out=out.ap, in_=in_.ap), reads=[in_], writes=[out])

    def recip(self, out, in_):
        self.P.add("dve", lambda e: e.reciprocal(out=out.ap, in_=in_.ap), reads=[in_], writes=[out])

    def memset(self, eng, out, val):
        self.P.add(eng, lambda e: e.memset(out.ap, val), writes=[out])

    def dma_in(self, q, out, src_ap, semkey):
        self.P.add(q, lambda e: e.dma_start(out=out.ap, in_=src_ap), writes=[out], dma=True, semkey=semkey)

    def dma_out(self, q, dst_ap, in_, semkey):
        self.P.add(q, lambda e: e.dma_start(out=dst_ap, in_=in_.ap), reads=[in_], dma=True, semkey=semkey)

    def wq_push(self, src_ap, shape, cast=True):
        self.wq.append((src_ap, shape, cast))

    def wq_get(self, ahead=2):
        i = self.wq_next
        self.wq_next += 1
        while self.wq_issued < min(len(self.wq), i + 1 + ahead):
            j = self.wq_issued
            src, shape, cast = self.wq[j]
            slot = j % NWB
            b = self.sb(shape, BF16, OFF_WB + slot * WB_BYTES, "wb")
            if isinstance(src, list):
                for t, s_ in enumerate(src):
                    self.dma_in("pool" if cast else "sp", b[:, t], s_, "wb%d_%d" % (slot, t))
            else:
                self.dma_in("pool" if cast else "sp", b.all(), src, "wb%d" % slot)
            self.wq_bufs[j] = b
            self.wq_issued += 1
        return self.wq_bufs.pop(i)


def al(x, a=64):
    return (x + a - 1) // a * a


def chunked(ap, c0, c1):
    return ap[:, c0:c1].rearrange("(k p) n -> p k n", p=128)


def build(stop_after=None):
    B = Builder(stop_after)
    nc = B.nc
    P = B.P

    def din(name, shape, dt=F32):
        return nc.dram_tensor(name, list(shape), dt, kind="ExternalInput").ap()

    xT = din("xT", [BPC, D, S])
    ctxT = din("ctxT", [BPC, D, LC])
    cT = din("cT", [128, 8, 3])
    w_ada = din("w_ada", [2, D, 6 * D])
    b_ada = din("b_ada", [128, 2, 48])
    ng = din("ng", [128, 2, 2, 8])
    ev_w_in = din("ev_w_in", [D, 1536])
    ev_cw = din("ev_cw", [128, 4, 31])
    ev_ln = din("ev_ln", [128, 2, 4])
    ev_w_out = din("ev_w_out", [D, D])
    od_w_in = din("od_w_in", [D, 1536])
    od_qkg = din("od_qkg", [128, 2])
    od_bd = din("od_bd", [128, 2, 128])
    od_ps = din("od_ps", [128, 2])
    od_w_out = din("od_w_out", [D, D])
    w_up = din("w_up", [2, D, 2 * DFF])
    fcw = din("fcw", [128, 2, 44, 3])
    w_dn = din("w_dn", [2, DFF, D])
    dftL = din("dftL", [S, 2, S], BF16)
    dftC = din("dftC", [LC, 2, LC], BF16)
    cs128 = din("cs128", [128, 256], BF16)
    ropeT = din("ropeT", [128, 2, S], BF16)
    ropeR = din("ropeR", [128, 128])
    poolc = din("poolc", [128, 2, 17])
    outT = nc.dram_tensor("outT", [BPC, D, S], F32, kind="ExternalOutput").ap()

    st = ExitStack()
    with st:
        ps_h = st.enter_context(nc.psum_tensor("ps", [128, 4096], F32))
        PS = Buf("ps", ps_h, [128, 4096], "ps", 0, 4)

        X = B.sb([128, 8, S], F32, OFF_X, "X")
        C = B.sb([128, 8, LC], F32, OFF_C, "C")
        HTX = B.sb([128, 8, S], BF16, OFF_HTX, "HTX")
        HTC = B.sb([128, 8, LC], BF16, OFF_HTC, "HTC")
        CVX = B.sb([128, 4, S], F32, OFF_HTX, "CVX")
        CVC = B.sb([128, 4, LC], F32, OFF_HTC, "CVC")

        ones = B.const([128, 128], F32, "ones")
        B.memset("pool", ones.all(), 1.0)
        onesb = B.const([128, 128], BF16, "onesb")
        B.memset("pool", onesb.all(), 1.0)
        cTs = B.const([128, 8, 3], F32, "cTs")
        scT = B.const([128, 8, 3], BF16, "scT")
        bada = B.const([128, 2, 48], F32, "bada")
        ngs = B.const([128, 2, 2, 8], F32, "ngs")
        mod = B.const([128, 2, 48, 3], F32, "mod")
        gsm = B.const([128, 2, 2, 8, 3], F32, "gsm")
        cs = B.const([128, 256], BF16, "cs")
        evcw = B.const([128, 4, 31], F32, "evcw")
        evln = B.const([128, 2, 4], F32, "evln")
        fcws = B.const([128, 2, 44, 3], F32, "fcws")
        identf = B.const([128, 128], F32, "identf")

        B.dma_in("sp", cTs.all(), cT, "c0")
        B.dma_in("sp", bada.all(), b_ada, "c1")
        B.dma_in("sp", ngs.all(), ng, "c2")
        B.dma_in("sp", cs.all(), cs128, "c3")
        B.dma_in("sp", evcw.all(), ev_cw, "c4")
        B.dma_in("sp", evln.all(), ev_ln, "c5")
        B.dma_in("sp", fcws.all(), fcw, "c6")
        identd = din("identd", [128, 128])
        B.dma_in("sp", identf.all(), identd, "c7")

        B.act(scT.all(), cTs.all(), AF.Silu)
        def mod_spec_blk(i, cb):
            B.wq_push(chunked(w_ada[i], cb * 512, (cb + 1) * 512), [128, 8, 512])

        def mod_blk(i, cb):
            pc = B.psum(512)
            wb = B.wq_get()
            for mt in range(4):
                o = PS[:, pc + mt * 3: pc + mt * 3 + 3]
                for k in range(8):
                    B.mm(o, wb[:, k, mt * 128:(mt + 1) * 128], scT[:, k, :], k == 0, k == 7)
            for col in range(3):
                B.tt("dve", mod[:, i, cb * 4:(cb + 1) * 4, col], PS[:, pc + col: pc + 12: 3], bada[:, i, cb * 4:(cb + 1) * 4], ALU.add)

        def mod_fin_n(i, n_):
            sc0 = 8 if n_ == 0 else 32
            for col in range(3):
                B.stt("dve", gsm[:, i, n_, :, col], mod[:, i, sc0:sc0 + 8, col], 1.0, ngs[:, i, n_, :],
                      ALU.add, ALU.mult)

        def mod_fin(i):
            for n_ in range(2):
                mod_fin_n(i, n_)

        def mod_specs(i):
            for cb in range(12):
                mod_spec_blk(i, cb)

        def mod_compute(i):
            for cb in range(12):
                mod_blk(i, cb)
            mod_fin(i)

        mod_specs(0)
        for cb_ in range(4):
            mod_blk(0, cb_)
        mod_fin_n(0, 0)

        def modv(i, which, j, col):
            base = {"sh1": 0, "sc1": 8, "g1": 16, "sh2": 24, "sc2": 32, "g2": 40}[which]
            return mod[:, i, base + j, col:col + 1]

        def mkseq(L, resid, hT, cv, tab):
            s_ = Seq()
            s_.L, s_.resid, s_.hT, s_.cv, s_.tab = L, resid, hT, cv, tab
            s_.TB = min(512, L)
            s_.NB = L // s_.TB
            s_.LT = L // 128
            return s_

        SX = mkseq(S, X, HTX, CVX, dftL)
        SC = mkseq(LC, C, HTC, CVC, dftC)

        def norm_mod(sq_, i, n_, col):
            L, TB = sq_.L, sq_.TB
            sqt = [B.scr([128, 512], BF16, 0, "sqt0"), B.scr([128, 512], BF16, 2048, "sqt1")]
            rt = B.scr([128, 512], F32, 4096, "rt")
            tt_ = [B.scr([128, 512], F32, 6144, "tt0"), B.scr([128, 512], F32, 8192, "tt1")]
            shn = "sh1" if n_ == 0 else "sh2"
            for tb in range(sq_.NB):
                c0, c1 = tb * TB, (tb + 1) * TB
                pc = B.psum(TB)
                pv = PS[:, pc:pc + TB]
                for j in range(8):
                    sq = sqt[j % 2][:, 0:TB]
                    B.act(sq, sq_.resid[:, j, c0:c1], AF.Square)
                    B.mm(pv, onesb.all(), sq, j == 0, j == 7)
                r = rt[:, 0:TB]
                B.act(r, pv, AF.Ln, scale=1.0 / D, bias=EPS)
                B.act(r, r, AF.Exp, scale=-0.5)
                for j in range(8):
                    t = tt_[j % 2][:, 0:TB]
                    B.stt("dve", t, sq_.resid[:, j, c0:c1], gsm[:, i, n_, j, col:col + 1], r, ALU.mult, ALU.mult)
                    B.act(sq_.hT[:, j, c0:c1], t, AF.Identity, bias=modv(i, shn, j, col))

        def proj(sq_, pc, lhs_fn, nk, rhs_fn):
            TB = sq_.TB
            for tb in range(sq_.NB):
                o = PS[:, pc + tb * TB: pc + (tb + 1) * TB]
                for k in range(nk):
                    B.mm(o, lhs_fn(k), rhs_fn(k, tb), k == 0, k == nk - 1)

        def resid_add(sq_, m, pc, gate):
            TB = sq_.TB
            L = sq_.L
            B.stt("dve", sq_.resid[:, m, :], PS[:, pc:pc + L], gate, sq_.resid[:, m, :], ALU.mult, ALU.add)

        def even_specs(sq_):
            B.wq_push(chunked(ev_w_in, 0, 512), [128, 8, 512])
            L, TB = sq_.L, sq_.TB
            nlt = min(4, sq_.LT)
            for tb in range(sq_.NB):
                for g in range(sq_.LT // nlt):
                    src = [sq_.tab[(g * nlt + t) * 128:(g * nlt + t + 1) * 128, :, tb * TB:(tb + 1) * TB]
                           for t in range(nlt)]
                    B.wq_push(src, [128, nlt, 2, TB], cast=False)
            for cb in range(1, 3):
                B.wq_push(chunked(ev_w_in, cb * 512, (cb + 1) * 512), [128, 8, 512])
            for cb in range(2):
                B.wq_push(chunked(ev_w_out, cb * 512, (cb + 1) * 512), [128, 8, 512])

        def even_mixer(sq_, col):
            L, TB, NB, LT = sq_.L, sq_.TB, sq_.NB, sq_.LT
            hT = sq_.hT
            OFF_WIN = SCR_BYTES - 8 * L * 2
            win = B.scr([128, 8, L], BF16, OFF_WIN, "win")
            aT = B.scr([128, 4, L], BF16, OFF_WIN, "aT")
            Bt = B.scr([128, LT, 1024], BF16, 0, "Bt")
            wA = B.wq_get()
            for m in range(4):
                pc = B.psum(L)
                proj(sq_, pc, lambda k: wA[:, k, m * 128:(m + 1) * 128], 8, lambda k, tb: hT[:, k, tb * TB:(tb + 1) * TB])
                B.act(aT[:, m, :], PS[:, pc:pc + L], AF.Identity)
            B.ck("m1")
            for lt in range(LT):
                pc = B.psum(1024)
                for m in range(4):
                    B.mm(PS[:, pc + m * 256: pc + (m + 1) * 256], aT[:, m, lt * 128:(lt + 1) * 128], cs.all(), True, True)
                B.copy("dve", Bt[:, lt, :], PS[:, pc:pc + 1024])
            B.ck("m2")
            nlt = min(4, LT)
            for tb in range(NB):
                pcs = [B.psum(TB) for _ in range(4)]
                for g in range(LT // nlt):
                    tb_ = B.wq_get()
                    for t in range(nlt):
                        lt = g * nlt + t
                        for m in range(4):
                            o = PS[:, pcs[m]:pcs[m] + TB]
                            B.mm(o, Bt[:, lt, m * 256: m * 256 + 128], tb_[:, t, 0, :], lt == 0, False)
                            B.mm(o, Bt[:, lt, m * 256 + 128: m * 256 + 256], tb_[:, t, 1, :], False, lt == LT - 1)
                for m in range(4):
                    B.act(win[:, m, tb * TB:(tb + 1) * TB], PS[:, pcs[m]:pcs[m] + TB], AF.Identity)
            B.ck("m3")
            bpad = B.scr([128, 4, L + 30], BF16, 0, "bpad")
            sig = B.scr([128, L], F32, al(4 * (L + 30) * 2), "sig")
            B.memset("pool", bpad[:, :, 0:15], 0.0)
            B.memset("pool", bpad[:, :, L + 15:L + 30], 0.0)
            for j in range(4):
                if j % 2 == 0:
                    wb = B.wq_get()
                jj = j % 2
                pg = B.psum(L)
                proj(sq_, pg, lambda k: wb[:, k, jj * 256:jj * 256 + 128], 8, lambda k, tb: hT[:, k, tb * TB:(tb + 1) * TB])
                B.act(sig.all(), PS[:, pg:pg + L], AF.Sigmoid)
                pu = B.psum(L)
                proj(sq_, pu, lambda k: wb[:, k, jj * 256 + 128:jj * 256 + 256], 8, lambda k, tb: hT[:, k, tb * TB:(tb + 1) * TB])
                B.tt("dve", bpad[:, j, 15:15 + L], PS[:, pu:pu + L], sig.all(), ALU.mult)
            B.ck("m4")
            o5 = al(4 * (L + 30) * 2)
            diag = B.scr([128, 31, 128], BF16, o5, "diag")
            o5 += 31 * 128 * 2
            sq2 = [B.scr([128, 512], BF16, o5, "sq2a"), B.scr([128, 512], BF16, o5 + 2048, "sq2b")]
            o5 += 4096
            rs = B.scr([128, 512], F32, o5, "rs")
            mu = B.scr([128, 512], F32, o5 + 2048, "mu")
            o5 += 4096
            yt = [B.scr([128, 512], F32, o5, "yta")] * 2
            o5 += 2048
            assert o5 <= OFF_WIN
            cv = sq_.cv
            for j in range(4):
                for k in range(31):
                    if k % 3 == 2:
                        B.ts("dve", diag[:, k, :], identf.all(), evcw[:, j, k:k + 1], ALU.mult)
                    else:
                        B.act(diag[:, k, :], identf.all(), AF.Identity, scale=evcw[:, j, k:k + 1])
                pcs5 = [B.psum(TB) for _ in range(NB)]
                for k in range(31):
                    for tb in range(NB):
                        B.mm(PS[:, pcs5[tb]:pcs5[tb] + TB], diag[:, k, :], bpad[:, j, tb * TB + k: tb * TB + k + TB],
                             k == 0, k == 30)
                for tb in range(NB):
                    B.act(cv[:, j, tb * TB:(tb + 1) * TB], PS[:, pcs5[tb]:pcs5[tb] + TB], AF.Identity)
            for tb in range(NB):
                c0, c1 = tb * TB, (tb + 1) * TB
                pm = B.psum(TB)
                pe2 = B.psum(TB)
                for j in range(4):
                    B.mm(PS[:, pm:pm + TB], ones.all(), cv[:, j, c0:c1], j == 0, j == 3)
                for j in range(4):
                    sq = sq2[j % 2][:, 0:TB]
                    B.act(sq, cv[:, j, c0:c1], AF.Square)
                    B.mm(PS[:, pe2:pe2 + TB], onesb.all(), sq, j == 0, j == 3)
                m_ = mu[:, 0:TB]
                r_ = rs[:, 0:TB]
                B.act(m_, PS[:, pm:pm + TB], AF.Identity, scale=1.0 / 512)
                B.tt("dve", r_, m_, m_, ALU.mult)
                B.stt("dve", r_, PS[:, pe2:pe2 + TB], 1.0 / 512, r_, ALU.mult, ALU.subtract)
                B.act(r_, r_, AF.Ln, scale=1.0, bias=EPS)
                B.act(r_, r_, AF.Exp, scale=-0.5)
                for j in range(4):
                    y = yt[j % 2][:, 0:TB]
                    B.tt("dve", y, cv[:, j, c0:c1], m_, ALU.subtract)
                    B.tt("dve", y, y, r_, ALU.mult)
                    B.act(win[:, 4 + j, c0:c1], y, AF.Silu, scale=evln[:, 0, j:j + 1], bias=evln[:, 1, j:j + 1])
            B.ck("m5")
            for m in range(8):
                if m % 4 == 0:
                    w_ = B.wq_get()
                mm_ = m % 4
                pc = B.psum(L)
                proj(sq_, pc, lambda k: w_[:, k, mm_ * 128:(mm_ + 1) * 128], 8, lambda k, tb: win[:, k, tb * TB:(tb + 1) * TB])
                B.ck("m7")
                resid_add(sq_, m, pc, modv(0, "g1", m, col))
                B.ck("m8")

        def ffn_specs(i, hook=None):
            nblk = 0
            for (g0, g1) in GROUPS:
                p0 = g0
                while p0 < g1:
                    n = min(2, g1 - p0)
                    B.wq_push(chunked(w_up[i], p0 * 256, (p0 + n) * 256), [128, 8, n * 256])
                    if hook is not None:
                        hook(nblk)
                    nblk += 1
                    p0 += n
                nk = g1 - g0
                for cb in range(4):
                    src = w_dn[i][g0 * 128:g1 * 128, cb * 256:(cb + 1) * 256].rearrange("(k p) n -> p k n", p=128)
                    B.wq_push(src, [128, nk, 256])

        def ffn(sq_, i, col, hook=None, cs_=None, ccol=None):
            L, TB, NB = sq_.L, sq_.TB, sq_.NB
            hT = sq_.hT
            nblk = 0
            PA, PB = 0, 2048
            a = B.scr([128, 8, L], BF16, 0, "a")
            o = 8 * L * 2
            ug = B.scr([128, L], F32, o, "ug")
            uv = B.scr([128, L], F32, o + 4 * L, "uv")
            sg = B.scr([128, L], BF16, o + 8 * L, "sg")
            o += 10 * L
            if cs_ is not None:
                Lc = cs_.L
                a_c = B.scr([128, 8, Lc], BF16, o, "a_c")
                o += 8 * Lc * 2
                ncb = 3
                ugc = [B.scr([128, Lc], F32, o + i_ * 10 * Lc, "ugc") for i_ in range(ncb)]
                uvc = [B.scr([128, Lc], F32, o + i_ * 10 * Lc + 4 * Lc, "uvc") for i_ in range(ncb)]
                sgc = [B.scr([128, Lc], BF16, o + i_ * 10 * Lc + 8 * Lc, "sgc") for i_ in range(ncb)]
                o += ncb * 10 * Lc
            assert o <= SCR_BYTES, (o, SCR_BYTES)

            def conv3(dst, pc, ch, n):
                w0 = fcws[:, i, ch, 0:1]
                w1 = fcws[:, i, ch, 1:2]
                w2 = fcws[:, i, ch, 2:3]
                B.act(dst.all(), PS[:, pc:pc + n], AF.Identity, scale=w1)
                B.stt("dve", dst[:, 1:n], PS[:, pc:pc + n - 1], w0, dst[:, 1:n], ALU.mult, ALU.add)
                B.stt("dve", dst[:, 0:n - 1], PS[:, pc + 1:pc + n], w2, dst[:, 0:n - 1], ALU.mult, ALU.add)

            for (g0, g1) in GROUPS:
                p0 = g0
                while p0 < g1:
                    n = min(2, g1 - p0)
                    wb = B.wq_get()
                    deferred = None
                    for q in range(n):
                        pr = p0 + q
                        proj(sq_, PA, lambda k: wb[:, k, q * 256: q * 256 + 128], 8, lambda k, tb: hT[:, k, tb * TB:(tb + 1) * TB])
                        conv3(ug, PA, 2 * pr, L)
                        B.act(sg.all(), ug.all(), AF.Silu)
                        proj(sq_, PB, lambda k: wb[:, k, q * 256 + 128: q * 256 + 256], 8, lambda k, tb: hT[:, k, tb * TB:(tb + 1) * TB])

                        def fin_v(pr=pr):
                            conv3(uv, PB, 2 * pr + 1, L)
                            B.tt("pool", a[:, pr - g0, :], sg.all(), uv.all(), ALU.mult)
                        if cs_ is not None and q == n - 1:
                            deferred = fin_v
                        else:
                            fin_v()
                    if cs_ is not None:
                        for q in range(n):
                            pr = p0 + q
                            bi = pr % ncb
                            pgc = PA + (2 * q) * 512
                            pvc = PA + (2 * q + 1) * 512
                            for k in range(8):
                                B.mm(PS[:, pgc:pgc + Lc], wb[:, k, q * 256: q * 256 + 128], cs_.hT[:, k, :], k == 0, k == 7)
                            for k in range(8):
                                B.mm(PS[:, pvc:pvc + Lc], wb[:, k, q * 256 + 128: q * 256 + 256], cs_.hT[:, k, :], k == 0, k == 7)
                            conv3(ugc[bi], pgc, 2 * pr, Lc)
                            B.act(sgc[bi].all(), ugc[bi].all(), AF.Silu)
                            conv3(uvc[bi], pvc, 2 * pr + 1, Lc)
                            B.tt("pool", a_c[:, pr - g0, :], sgc[bi].all(), uvc[bi].all(), ALU.mult)
                        deferred()
                    if hook is not None:
                        B.ps_ptr = 3
                        hook(nblk)
                    nblk += 1
                    p0 += n
                nk = g1 - g0
                for cb in range(4):
                    wd = B.wq_get()
                    for h in range(2):
                        m = cb * 2 + h
                        pc = PA if h == 0 else PB
                        proj(sq_, pc, lambda k: wd[:, k, h * 128:(h + 1) * 128], nk, lambda k, tb: a[:, k, tb * TB:(tb + 1) * TB])
                        resid_add(sq_, m, pc, modv(i, "g2", m, col))
                    if cs_ is not None:
                        for h in range(2):
                            m = cb * 2 + h
                            pc = PA + h * 512
                            for k in range(nk):
                                B.mm(PS[:, pc:pc + Lc], wd[:, k, h * 128:(h + 1) * 128], a_c[:, k, :], k == 0, k == nk - 1)
                            resid_add(cs_, m, pc, modv(i, "g2", m, ccol))
            B.ps_ptr = 0

        R0 = B.const([128, 128], F32, "R0")
        Rq = B.const([128, 128], F32, "Rq")
        Rk = B.const([128, 128], F32, "Rk")
        qkg = B.const([128, 2], F32, "qkg")
        bdm = B.const([128, 2, 128], BF16, "bdm")
        bdf = B.scr([128, 2, 128], F32, 0, "bdf")
        pscl = B.const([128, 2], F32, "pscl")
        plc = B.const([128, 2, 17], F32, "plc")
        bones = B.const([128, 128], F32, "bones")
        etmp = B.const([128, 8], F32, "etmp")
        B.dma_in("sp", R0.all(), ropeR, "d0")
        B.dma_in("sp", qkg.all(), od_qkg, "d1")
        B.dma_in("sp", bdf.all(), od_bd, "d2")
        B.dma_in("sp", pscl.all(), od_ps, "d3")
        B.dma_in("sp", plc.all(), poolc, "d4")
        B.copy("dve", bdm.all(), bdf.all())
        B.ts("dve", Rq.all(), R0.all(), qkg[:, 0:1], ALU.mult)
        B.ts("dve", Rk.all(), R0.all(), qkg[:, 1:2], ALU.mult)
        B.memset("pool", bones.all(), 0.0)
        B.memset("pool", bones[0:64, 0:64], 1.0)
        B.memset("pool", bones[64:128, 64:128], 1.0)

        def odd_specs():
            B.wq_push(chunked(od_w_in, 1024, 1536), [128, 8, 512])
            B.wq_push(chunked(od_w_in, 0, 512), [128, 8, 512])
            B.wq_push(chunked(od_w_in, 512, 1024), [128, 8, 512])
            for cb in range(2):
                B.wq_push(chunked(od_w_out, cb * 512, (cb + 1) * 512), [128, 8, 512])

        def odd_mixer(col):
            L, TB, NB = S, 512, 4
            NKT = 18
            o = 0
            poolout = B.scr([128, 2, S], BF16, o, "poolout"); o += 2 * S * 2
            Vp = B.scr([128, NKT, 4, 128], BF16, o, "Vp"); o += NKT * 4 * 128 * 2
            o_after_v = o
            q_sb = B.scr([128, 6, S], BF16, o, "q_sb"); o += 6 * S * 2
            k_sb = B.scr([128, 2, S + LC], BF16, o, "k_sb"); o += 2 * (S + LC) * 2
            rtab = B.scr([128, 2, S], BF16, o, "rtab")
            rtab_off = o
            o += 2 * S * 2
            assert o <= SCR_BYTES, (o, SCR_BYTES)
            attn = B.sb([128, 8, S], BF16, OFF_HTX, "attn")
            o2 = o_after_v
            upad = B.scr([128, S + 16], F32, o2, "upad"); o2 += al((S + 16) * 4)
            bA = B.scr([128, S + 16], F32, o2, "bA"); o2 += al((S + 16) * 4)
            bB = B.scr([128, S + 16], F32, o2, "bB"); o2 += al((S + 16) * 4)
            pooled = B.scr([128, 2, S], BF16, o2, "pooled"); o2 += 2 * S * 2
            assert o2 <= SCR_BYTES, (o2, SCR_BYTES)
            tmpc = [B.sb([128, 512], F32, OFF_C + i_ * 2048, "tmpc") for i_ in range(4)]
            hx = lambda k, tb: HTX[:, k, tb * TB:(tb + 1) * TB]

            w2 = B.wq_get()
            B.memset("pool", Vp.all(), 1.0)
            B.memset("pool", upad[:, 0:8], 0.0)
            B.memset("pool", upad[:, S + 8:S + 16], 0.0)
            W = S + 16
            for ch in range(2):
                pc = B.psum(L)
                proj(SX, pc, lambda k: w2[:, k, 256 + ch * 128: 256 + (ch + 1) * 128], 8, hx)
                B.act(upad[:, 8:8 + S], PS[:, pc:pc + L], AF.Identity)
                B.tt("dve", bA[:, 1:W], upad[:, 0:W - 1], upad[:, 1:W], ALU.add)
                B.tt("dve", bB[:, 2:W - 1], bA[:, 1:W - 2], bA[:, 3:W], ALU.add)
                if ch == 0:
                    srcs = [(0, bA), (64, bB)]
                else:
                    B.tt("dve", bA[:, 4:W - 3], bB[:, 2:W - 5], bB[:, 6:W - 1], ALU.add)
                    B.tt("dve", bB[64:128, 8:W - 8], bA[64:128, 4:W - 12], bA[64:128, 12:W - 4], ALU.add)
                    srcs = [(0, bA), (64, bB)]
                for (p0, src) in srcs:
                    ps_ = slice(p0, p0 + 64)
                    B.stt("dve", pooled[ps_, ch, :], src[ps_, 8:8 + S], plc[ps_, ch, 0:1], upad[ps_, 8:8 + S],
                          ALU.mult, ALU.subtract)
                    for (c0, e0) in ((0, 1), (S - 8, 9)):
                        B.tt("dve", etmp[ps_, :], src[ps_, 8 + c0:16 + c0], plc[ps_, ch, e0:e0 + 8], ALU.mult)
                        B.tt("dve", pooled[ps_, ch, c0:c0 + 8], etmp[ps_, :], upad[ps_, 8 + c0:16 + c0], ALU.subtract)
            for kt in range(NKT):
                pc = B.psum(256)
                for k in range(8):
                    lh = HTC[:, k, kt * 128:(kt + 1) * 128] if kt < 2 else HTX[:, k, (kt - 2) * 128:(kt - 1) * 128]
                    B.mm(PS[:, pc:pc + 256], lh, w2[:, k, 0:256], k == 0, k == 7)
                for h in range(4):
                    bs = 64 * (h % 2)
                    if True:
                        B.act(Vp[:, kt, h, bs:bs + 64], PS[:, pc + h * 64:pc + (h + 1) * 64], AF.Identity)
                    else:
                        B.copy("dve", Vp[:, kt, h, bs:bs + 64], PS[:, pc + h * 64:pc + (h + 1) * 64])
            for ch in range(2):
                pc = B.psum(L)
                proj(SX, pc, lambda k: bdm[:, ch, :], 1, lambda k, tb: pooled[:, ch, tb * TB:(tb + 1) * TB])
                B.act(poolout[:, ch, :], PS[:, pc:pc + L], AF.Identity, scale=pscl[:, ch:ch + 1])
            B.ck("o2")
            B.dma_in("sp", rtab.all(), ropeT, "rtab")

            rope_n = [0]

            def rope(pv, n, gcol, Rm, outv, tok0, use_rope):
                par = rope_n[0] % 2
                rope_n[0] += 1
                qs = tmpc[2 * par][:, 0:n]
                qq = tmpc[2 * par + 1][:, 0:n]
                B.act(qs, pv, AF.Identity)
                B.act(qq, pv, AF.Square)
                pm = B.psum(n)
                rt_ = PS[:, pm:pm + n]
                B.mm(rt_, bones.all(), qq, True, True)
                if use_rope:
                    pq = B.psum(n)
                    B.mm(PS[:, pq:pq + n], Rm.all(), qs, True, True)
                B.act(rt_, rt_, AF.Ln, scale=1.0 / 64, bias=EPS)
                B.act(rt_, rt_, AF.Exp, scale=-0.5)
                if use_rope:
                    B.tt("dve", qq, PS[:, pq:pq + n], rtab[:, 1, tok0:tok0 + n], ALU.mult)
                    B.stt("dve", qs, qs, qkg[:, gcol:gcol + 1], rtab[:, 0, tok0:tok0 + n], ALU.mult, ALU.mult)
                    B.tt("pool", qs, qs, qq, ALU.add)
                    B.tt("dve", outv, qs, rt_, ALU.mult)
                else:
                    B.stt("dve", outv, qs, qkg[:, gcol:gcol + 1], rt_, ALU.mult, ALU.mult)

            w0 = B.wq_get()
            for j in range(4):
                pc = B.psum(L)
                proj(SX, pc, lambda k: w0[:, k, j * 128:(j + 1) * 128], 8, hx)
                for tb in range(NB):
                    rope(PS[:, pc + tb * TB:pc + (tb + 1) * TB], TB, 0, Rq, q_sb[:, j, tb * TB:(tb + 1) * TB], tb * TB, True)
            w1 = B.wq_get()
            for j in range(4, 6):
                pc = B.psum(L)
                proj(SX, pc, lambda k: w1[:, k, (j - 4) * 128:(j - 3) * 128], 8, hx)
                for tb in range(NB):
                    rope(PS[:, pc + tb * TB:pc + (tb + 1) * TB], TB, 0, Rq, q_sb[:, j, tb * TB:(tb + 1) * TB], tb * TB, True)
            for c_ in range(2):
                pc = B.psum(L)
                proj(SX, pc, lambda k: w1[:, k, 256 + c_ * 128:256 + (c_ + 1) * 128], 8, hx)
                for tb in range(NB):
                    rope(PS[:, pc + tb * TB:pc + (tb + 1) * TB], TB, 1, Rk,
                         k_sb[:, c_, LC + tb * TB:LC + (tb + 1) * TB], tb * TB, True)
                pc = B.psum(LC)
                for k in range(8):
                    B.mm(PS[:, pc:pc + LC], w1[:, k, 256 + c_ * 128:256 + (c_ + 1) * 128], HTC[:, k, :], k == 0, k == 7)
                rope(PS[:, pc:pc + LC], LC, 1, Rk, k_sb[:, c_, 0:LC], 0, False)
            B.ck("o3")
            ACC = 0
            STS = [1024, 2048, 3072]
            accS = [B.sb([128, 1024], F32, OFF_C, "accS0"), B.sb([128, 1024], F32, OFF_C + 4096, "accS1")]
            pT3 = [B.scr([128, 1024], BF16, rtab_off + i_ * 2048, "pT3") for i_ in range(3)]
            dsh = B.scr([128, 512], F32, rtab_off + 6144, "dsh")
            its = []
            for j in range(6):
                for qb in range(4):
                    for kt in range(NKT):
                        its.append((j, qb, kt))
            kvs = [[QORDER[2 * j + hb] // 3 for hb in range(2)] for j in range(6)]
            for j in range(6):
                assert kvs[j][0] % 2 == 0 and kvs[j][1] % 2 == 1

            def qk(i):
                j, qb, kt = its[i]
                st_ = STS[i % 3]
                for hb in range(2):
                    pr = slice(64 * hb, 64 * hb + 64)
                    B.mm(PS[:, st_ + hb * 512: st_ + (hb + 1) * 512], k_sb[pr, kvs[j][hb] // 2, kt * 128:(kt + 1) * 128],
                         q_sb[pr, j, qb * 512:(qb + 1) * 512], True, True)

            qk(0)
            qk(1)
            for i, (j, qb, kt) in enumerate(its):
                if i + 2 < len(its):
                    qk(i + 2)
                st_ = STS[i % 3]
                pT = pT3[i % 3]
                B.act(pT.all(), PS[:, st_:st_ + 1024], AF.Exp, scale=0.125)
                for hb in range(2):
                    B.mm(PS[:, ACC + hb * 512:ACC + (hb + 1) * 512], Vp[:, kt, kvs[j][hb], :], pT[:, hb * 512:(hb + 1) * 512],
                         kt == 0, kt == NKT - 1)
                if kt == NKT - 1:
                    g = i // NKT
                    aS = accS[g % 2]
                    B.copy("dve", aS.all(), PS[:, ACC:ACC + 1024])
                    B.P.add("sp", (lambda aS=aS: (lambda e: e.dma_start(out=dsh[0:64, :].ap, in_=aS[64:128, 0:512].ap)))(),
                            reads=[aS[64:128, 0:512]], writes=[dsh[0:64, :]], dma=True, semkey="dshA")
                    B.P.add("sp", (lambda aS=aS: (lambda e: e.dma_start(out=dsh[64:128, :].ap, in_=aS[0:64, 512:1024].ap)))(),
                            reads=[aS[0:64, 512:1024]], writes=[dsh[64:128, :]], dma=True, semkey="dshB")
                    B.recip(dsh.all(), dsh.all())
                    B.tt("dve", attn[0:64, j, qb * 512:(qb + 1) * 512], aS[0:64, 0:512], dsh[0:64, :], ALU.mult)
                    B.tt("dve", attn[64:128, j, qb * 512:(qb + 1) * 512], aS[64:128, 512:1024], dsh[64:128, :], ALU.mult)
            B.ps_ptr = 0
            for m in range(8):
                if m % 4 == 0:
                    w_ = B.wq_get()
                mm_ = m % 4
                pc = B.psum(L)
                proj(SX, pc, lambda k: w_[:, k, mm_ * 128:(mm_ + 1) * 128], 8,
                     lambda k, tb: (attn[:, k, tb * TB:(tb + 1) * TB] if k < 6 else poolout[:, k - 6, tb * TB:(tb + 1) * TB]))
                resid_add(SX, m, pc, modv(1, "g1", m, col))

        try:
            for b in range(BPC):
                for j in range(8):
                    B.dma_in("sp", X[:, j, :], xT[b, j * 128:(j + 1) * 128, :], "xin%d" % j)
                B.dma_in("sp", C.all(), ctxT[b].rearrange("(j p) t -> p j t", p=128), "cin")
                B.ck("load")
                even_specs(SX)
                even_specs(SC)
                ffn_specs(0, (lambda n: mod_spec_blk(1, n)) if b == 0 else None)
                odd_specs()
                ffn_specs(1)
                norm_mod(SX, 0, 0, b)
                if b == 0:
                    for cb_ in range(4, 12):
                        mod_blk(0, cb_)
                    mod_fin_n(0, 1)
                B.ck("norm")
                even_mixer(SX, b)
                B.ck("mixer")
                norm_mod(SX, 0, 1, b)
                B.ck("ffn")
                norm_mod(SC, 0, 0, 2)
                even_mixer(SC, 2)
                norm_mod(SC, 0, 1, 2)
                ffn(SX, 0, b, (lambda n: mod_blk(1, n)) if b == 0 else None, SC, 2)
                if b == 0:
                    mod_fin(1)
                B.ck("l0")
                norm_mod(SX, 1, 0, b)
                norm_mod(SC, 1, 0, 2)
                odd_mixer(b)
                B.ck("mixer1")
                norm_mod(SX, 1, 1, b)
                ffn(SX, 1, b)
                for j in range(8):
                    B.dma_out("sp", outT[b, j * 128:(j + 1) * 128, :], X[:, j, :], "xout%d" % j)
        except _Stop:
            for j in range(8):
                B.dma_out("sp", outT[0, j * 128:(j + 1) * 128, :], X[:, j, :], "xout%d" % j)
            if B.stop_after == "norm":
                hdbg = nc.dram_tensor("hdbg", [128, 8, S], BF16, kind="ExternalOutput").ap()
                B.dma_out("sp", hdbg, HTX.all(), "hdbg")
            B.wq_next = len(B.wq)

        assert B.wq_next == len(B.wq), (B.wq_next, len(B.wq))
        P.finalize(st)
        import collections
        cnt = collections.Counter((op.eng, op.dma) for op in P.ops)
        sig = collections.Counter(op.eng for op in P.ops if op.signal and not op.dma)
        nw = sum(len(op.waits) for op in P.ops)
        print("OPS", dict(cnt), "SIGNALS", dict(sig), "WAITS", nw, "SEMS", len(P.sems), flush=True)
        P.emit()
    return nc


def _fm(v, nchunk):
    v = np.asarray(v, np.float32)
    lead = v.shape[:-1]
    v = v.reshape(lead + (nchunk, 128))
    return np.ascontiguousarray(np.moveaxis(v, -1, 0))


def _consts():
    bf = ml_dtypes.bfloat16
    out = {}
    for name, L in (("dftL", S), ("dftC", LC)):
        l = np.arange(L, dtype=np.int64)
        ang = 2.0 * np.pi * ((l[:, None] * l[None, :]) % L).astype(np.float64) / L
        t = np.stack([np.cos(ang), -np.sin(ang)], axis=1) / math.sqrt(L)
        out[name] = t.astype(np.float32).astype(bf)
    c = np.arange(128, dtype=np.int64)
    ang = 2.0 * np.pi * ((c[:, None] * c[None, :]) % 128).astype(np.float64) / 128
    out["cs128"] = (np.concatenate([np.cos(ang), np.sin(ang)], axis=1) / math.sqrt(128)).astype(np.float32).astype(bf)
    out["identd"] = np.eye(128, dtype=np.float32)
    t = np.arange(S)
    freqs = (10000.0 ** (-np.arange(16, dtype=np.float32) / 16)).astype(np.float32)
    rows = (t // 64).astype(np.float32)
    cols = (t % 64).astype(np.float32)
    rt = np.zeros((128, 2, S), np.float32)
    R = np.zeros((128, 128), np.float32)
    for p in range(128):
        dd = p % 64
        blk = dd // 32
        i = dd % 16
        half = (dd % 32) // 16
        pos = rows if blk == 0 else cols
        ang_ = (pos * freqs[i]).astype(np.float32)
        rt[p, 0] = np.cos(ang_)
        rt[p, 1] = np.sin(ang_)
        partner = p + 16 if half == 0 else p - 16
        R[partner, p] = -1.0 if half == 0 else 1.0
    out["ropeT"] = rt.astype(bf)
    out["ropeR"] = R
    pc = np.zeros((128, 2, 17), np.float32)
    for ch in range(2):
        for hp in range(2):
            w = (2, 4, 8, 16)[ch * 2 + hp]
            sl = slice(hp * 64, hp * 64 + 64)
            pc[sl, ch, 0] = 1.0 / w
            for e in range(8):
                tt_ = e
                lo = max(tt_ - w // 2, 0)
                hi = min(tt_ + w // 2 - 1, S - 1) + 1
                pc[sl, ch, 1 + e] = 1.0 / (hi - lo)
                tt_ = S - 8 + e
                lo = max(tt_ - w // 2, 0)
                hi = min(tt_ + w // 2 - 1, S - 1) + 1
                pc[sl, ch, 9 + e] = 1.0 / (hi - lo)
    out["poolc"] = pc
    return out


_NC_CACHE = {}
QORDER = [0, 3, 1, 4, 2, 5, 6, 9, 7, 10, 8, 11]


def _prep(inputs):
    f = lambda k: np.asarray(inputs[k], np.float32)
    x, c, ctx, c_ctx = f("x"), f("c"), f("ctx"), f("c_ctx")
    shared = {}
    shared["w_ada"] = np.ascontiguousarray(f("w_ada"))
    shared["b_ada"] = np.ascontiguousarray(np.transpose(f("b_ada").reshape(2, 48, 128), (2, 0, 1)))
    ngv = np.stack([f("norm1_g"), f("norm2_g")], axis=1)
    shared["ng"] = np.ascontiguousarray(np.transpose(ngv.reshape(2, 2, 8, 128), (3, 0, 1, 2)))
    w_in = f("ev_w_in")[0]
    cols = list(range(512))
    for j in range(4):
        cols += list(range(1024 + j * 128, 1024 + (j + 1) * 128))
        cols += list(range(512 + j * 128, 512 + (j + 1) * 128))
    shared["ev_w_in"] = np.ascontiguousarray(w_in[:, cols])
    shared["ev_cw"] = np.ascontiguousarray(np.transpose(f("ev_conv_w")[0].reshape(31, 4, 128), (2, 1, 0)))
    shared["ev_ln"] = np.ascontiguousarray(np.transpose(
        np.stack([f("ev_ln_g")[0], f("ev_ln_b")[0]], 0).reshape(2, 4, 128), (2, 0, 1)))
    shared["ev_w_out"] = np.ascontiguousarray(f("ev_w_out")[0])
    ow = f("od_w_in")[0]
    qcols = []
    for h in QORDER:
        qcols += list(range(h * 64, (h + 1) * 64))
    shared["od_w_in"] = np.ascontiguousarray(np.concatenate([ow[:, qcols], ow[:, 768:]], axis=1))
    shared["od_qkg"] = np.ascontiguousarray(np.stack([np.tile(f("od_q_g")[0], 2), np.tile(f("od_k_g")[0], 2)], axis=1))
    pw = f("od_pool_w")[0]
    bd = np.zeros((128, 2, 128), np.float32)
    for ch in range(2):
        bd[0:64, ch, 0:64] = pw[2 * ch]
        bd[64:128, ch, 64:128] = pw[2 * ch + 1]
    shared["od_bd"] = bd
    shared["od_ps"] = np.ascontiguousarray(f("od_pool_scale")[0].reshape(2, 128).T)
    wo = f("od_w_out")[0]
    shared["od_w_out"] = np.ascontiguousarray(np.concatenate([wo[qcols, :], wo[768:, :]], axis=0))
    wu = f("ffn_w_up")
    fc = f("ffn_conv_w")
    ucols = []
    for pr in range(NPAIR):
        ucols += list(range(pr * 128, (pr + 1) * 128))
        ucols += list(range(DFF + pr * 128, DFF + (pr + 1) * 128))
    shared["w_up"] = np.ascontiguousarray(wu[:, :, ucols])
    fcp = fc[:, :, ucols]
    shared["fcw"] = np.ascontiguousarray(np.transpose(fcp.reshape(2, 3, 44, 128), (3, 0, 2, 1)))
    shared["w_dn"] = np.ascontiguousarray(f("ffn_w_down"))
    shared.update(_consts())
    maps = []
    for core in range(NCORES):
        bs = [core * BPC + i for i in range(BPC)]
        m = dict(shared)
        m["xT"] = np.ascontiguousarray(np.transpose(x[bs], (0, 2, 1)))
        m["ctxT"] = np.ascontiguousarray(np.transpose(ctx[bs], (0, 2, 1)))
        cc = np.stack([c[bs[0]], c[bs[1]], c_ctx], axis=1)
        m["cT"] = np.ascontiguousarray(np.transpose(cc.reshape(8, 128, 3), (1, 0, 2)))
        maps.append(m)
    return maps


def kernel(**inputs):
    maps = _prep(inputs)
    if "nc" not in _NC_CACHE:
        _NC_CACHE["nc"] = build()
    nc = _NC_CACHE["nc"]
    res = run_bass_kernel_spmd(nc, maps, core_ids=list(range(NCORES)))
    outs = []
    for core in range(NCORES):
        o = np.asarray(res.results[core]["outT"], np.float32)
        outs.append(np.transpose(o, (0, 2, 1)))
    return np.ascontiguousarray(np.concatenate(outs, axis=0))
```

```python
import math
from contextlib import ExitStack

import numpy as np
import ml_dtypes

import concourse.bass as bass
import concourse.mybir as mybir
from concourse.bass_utils import run_bass_kernel_spmd

F32 = mybir.dt.float32
BF16 = mybir.dt.bfloat16
AF = mybir.ActivationFunctionType
ALU = mybir.AluOpType

D = 1024
S = 2048
LC = 256
NCORES = 8
BPC = 2
DFF = 2816
NPAIR = 22
EPS = 1e-6
GROUPS = [(0, 8), (8, 15), (15, 22)]


class Buf:
    def __init__(self, name, handle, shape, space="sb", base=0, esize=4):
        self.name, self.space, self.base, self.esize = name, space, base, esize
        self.h = handle
        self.shape = list(shape)
        st = [1] * len(shape)
        for i in range(len(shape) - 2, 0, -1):
            st[i] = st[i + 1] * shape[i + 1]
        self.strides = st

    def __getitem__(self, idx):
        if not isinstance(idx, tuple):
            idx = (idx,)
        idx = tuple(idx) + (slice(None),) * (len(self.shape) - len(idx))
        lo = hi = 0
        for d in range(1, len(self.shape)):
            i = idx[d]
            if isinstance(i, int):
                a, b = i, i
            else:
                a, b, s = i.indices(self.shape[d])
                n = max(0, (b - a + s - 1) // s)
                b = a + (n - 1) * s
            lo += a * self.strides[d]
            hi += b * self.strides[d]
        return View(self, self.h[idx], lo, hi + 1)

    def all(self):
        return self[tuple(slice(None) for _ in self.shape)]


class View:
    __slots__ = ("buf", "ap", "lo", "hi")

    def __init__(self, buf, ap, lo, hi):
        self.buf, self.ap, self.lo, self.hi = buf, ap, lo, hi

    @property
    def res(self):
        b = self.buf
        return (b.space, b.base + self.lo * b.esize, b.base + self.hi * b.esize)


class Op:
    __slots__ = ("eng", "fn", "deps", "signal", "dma", "semkey", "cnt", "waits")


class Prog:
    ENGS = ("pe", "act", "dve", "pool", "sp")

    def __init__(self, nc):
        self.nc = nc
        self.ops = []
        self.hist = {}

    def add(self, eng, fn, reads=(), writes=(), dma=False, semkey=None):
        idx = len(self.ops)
        reads = [r.res for r in reads if r is not None]
        writes = [w.res for w in writes if w is not None]
        deps = set()
        hist = self.hist
        for (name, lo, hi) in reads:
            for r in hist.get(name, ()):
                if r[3] and r[0] < hi and lo < r[1]:
                    deps.add(r[2])
        for (name, lo, hi) in writes:
            for r in hist.get(name, ()):
                if r[0] < hi and lo < r[1]:
                    deps.add(r[2])
        for (name, lo, hi) in writes:
            lst = hist.setdefault(name, [])
            lst[:] = [r for r in lst if not (lo <= r[0] and r[1] <= hi)]
            lst.append((lo, hi, idx, True, eng, dma))
        for (name, lo, hi) in reads:
            lst = hist.setdefault(name, [])
            if not dma:
                lst[:] = [r for r in lst if not ((not r[3]) and r[4] == eng and (not r[5])
                                                 and lo <= r[0] and r[1] <= hi)]
            lst.append((lo, hi, idx, False, eng, dma))
        op = Op()
        op.eng, op.fn, op.dma, op.semkey = eng, fn, dma, semkey
        op.deps, op.signal, op.cnt, op.waits = deps, dma, 0, None
        self.ops.append(op)
        return idx

    def finalize(self, stack):
        nc, ops = self.nc, self.ops

        def skip(p, op):
            return (not p.dma) and (not op.dma) and p.eng == "pe" and op.eng == "pe"

        for op in ops:
            for d in op.deps:
                p = ops[d]
                if not p.dma and not skip(p, op):
                    p.signal = True
        sems = {}

        def sem(key):
            if key not in sems:
                sems[key] = stack.enter_context(nc.semaphore("s%d" % len(sems)))
            return sems[key]

        cnt = {}
        for op in ops:
            key = ("d", op.semkey) if op.dma else ("e", op.eng)
            if op.signal:
                cnt[key] = cnt.get(key, 0) + (16 if op.dma else 1)
                op.cnt = cnt[key]
        known = {e: {} for e in self.ENGS}
        for op in ops:
            w = {}
            for d in op.deps:
                p = ops[d]
                if skip(p, op):
                    continue
                key = ("d", p.semkey) if p.dma else ("e", p.eng)
                if p.cnt > w.get(key, 0):
                    w[key] = p.cnt
            kn = known[op.eng]
            out = []
            for key, v in w.items():
                if kn.get(key, 0) >= v:
                    continue
                kn[key] = v
                out.append((sem(key), v))
            op.waits = out
            if op.signal:
                sem(("d", op.semkey) if op.dma else ("e", op.eng))
        self.sems = sems

    def emit(self):
        nc = self.nc
        by = {e: [op for op in self.ops if op.eng == e] for e in self.ENGS}
        sems = self.sems

        def run(e, lst):
            for op in lst:
                for (s, v) in op.waits:
                    e.wait_ge(s, v)
                ins = op.fn(e)
                if op.signal:
                    key = ("d", op.semkey) if op.dma else ("e", op.eng)
                    ins.then_inc(sems[key], 16 if op.dma else 1)

        with nc.Block() as block:
            @block.tensor
            def _(e):
                run(e, by["pe"])

            @block.scalar
            def _(e):
                run(e, by["act"])

            @block.vector
            def _(e):
                run(e, by["dve"])

            @block.gpsimd
            def _(e):
                run(e, by["pool"])

            @block.sync
            def _(e):
                run(e, by["sp"])
                fin = {}
                for op in self.ops:
                    if op.dma:
                        fin[("d", op.semkey)] = op.cnt
                for key, v in fin.items():
                    e.wait_ge(sems[key], v)


SB_LO = 16384
SB_HI = 229376 - 256

OFF_X = SB_LO
OFF_C = OFF_X + 8 * S * 4
OFF_HTX = OFF_C + 8 * LC * 4
OFF_HTC = OFF_HTX + 8 * S * 2
OFF_WB = OFF_HTC + 8 * LC * 2
NWB = 3
WB_BYTES = 8192
OFF_CONST = OFF_WB + NWB * WB_BYTES
CONST_BYTES = 8704
OFF_SCR = OFF_CONST + CONST_BYTES
SCR_BYTES = SB_HI - OFF_SCR


class Seq:
    pass


class _Stop(Exception):
    pass


class Builder:
    def __init__(self, stop_after=None):
        self.stop_after = stop_after
        self.nc = bass.Bass("TRN2", target_bir_lowering=False)
        self.P = Prog(self.nc)
        self.uid = 0
        self.ps_ptr = 0
        self.wb_ptr = 0
        self.const_ptr = 0
        self.wq = []
        self.wq_issued = 0
        self.wq_next = 0
        self.wq_bufs = {}

    def ck(self, name):
        if self.stop_after == name:
            raise _Stop()

    def sb(self, shape, dt, off, name=None):
        self.uid += 1
        name = "%s_%d" % (name or "t", self.uid)
        es = 4 if dt == F32 else 2
        n = 1
        for s_ in shape[1:]:
            n *= s_
        assert off >= SB_LO and off + n * es <= SB_HI, (name, off, n * es)
        h = self.nc.alloc_sbuf_tensor_at(name, list(shape), dt, offset=off)
        return Buf(name, h, shape, "sb", off, es)

    def scr(self, shape, dt, off, name=None):
        es = 4 if dt == F32 else 2
        n = 1
        for s_ in shape[1:]:
            n *= s_
        assert off + n * es <= SCR_BYTES, (name, off, n * es, SCR_BYTES)
        return self.sb(shape, dt, OFF_SCR + off, name)

    def const(self, shape, dt, name=None):
        es = 4 if dt == F32 else 2
        n = 1
        for s_ in shape[1:]:
            n *= s_
        off = (self.const_ptr + 31) // 32 * 32
        assert off + n * es <= CONST_BYTES, ("const overflow", name)
        self.const_ptr = off + n * es
        return self.sb(shape, dt, OFF_CONST + off, name)

    def psum(self, ncols):
        nb = (ncols + 511) // 512
        nb = {1: 1, 2: 2, 3: 4, 4: 4}[nb]
        p = (self.ps_ptr + nb - 1) // nb * nb
        if p + nb > 8:
            p = 0
        self.ps_ptr = (p + nb) % 8
        return p * 512

    def mm(self, out, lhsT, rhs, start, stop):
        self.P.add("pe", lambda e: e.matmul(out.ap, lhsT=lhsT.ap, rhs=rhs.ap, start=start, stop=stop),
                   reads=[lhsT, rhs], writes=[out])

    def act(self, out, in_, func, scale=1.0, bias=None, eng="act"):
        rd = [in_]
        kw = {}
        if isinstance(scale, View):
            rd.append(scale)
            kw["scale"] = scale.ap
        else:
            kw["scale"] = float(scale)
        if isinstance(bias, View):
            rd.append(bias)
            kw["bias"] = bias.ap
        elif bias is not None:
            kw["bias"] = float(bias)
        self.P.add("act", lambda e: e.activation(out=out.ap, in_=in_.ap, func=func, **kw),
                   reads=rd, writes=[out])

    def tt(self, eng, out, in0, in1, op):
        self.P.add(eng, lambda e: e.tensor_tensor(out=out.ap, in0=in0.ap, in1=in1.ap, op=op),
                   reads=[in0, in1], writes=[out])

    def ts(self, eng, out, in0, s1, op0, s2=None, op1=None):
        rd = [in0]
        a1 = s1.ap if isinstance(s1, View) else float(s1)
        if isinstance(s1, View):
            rd.append(s1)
        a2 = None
        if s2 is not None:
            a2 = s2.ap if isinstance(s2, View) else float(s2)
            if isinstance(s2, View):
                rd.append(s2)
        if op1 is None:
            self.P.add(eng, lambda e: e.tensor_scalar(out=out.ap, in0=in0.ap, scalar1=a1, scalar2=None, op0=op0),
                       reads=rd, writes=[out])
        else:
            self.P.add(eng, lambda e: e.tensor_scalar(out=out.ap, in0=in0.ap, scalar1=a1, scalar2=a2, op0=op0, op1=op1),
                       reads=rd, writes=[out])

    def stt(self, eng, out, in0, scalar, in1, op0, op1):
        rd = [in0, in1]
        a = scalar.ap if isinstance(scalar, View) else float(scalar)
        if isinstance(scalar, View):
            rd.append(scalar)
        self.P.add(eng, lambda e: e.scalar_tensor_tensor(out=out.ap, in0=in0.ap, scalar=a, in1=in1.ap, op0=op0, op1=op1),
                   reads=rd, writes=[out])

    def copy(self, eng, out, in_):
        self.P.add(eng, lambda e: e.tensor_copy(out=out.ap, in_=in_.ap), reads=[in_], writes=[out])

    def recip(self, out, in_):
        self.P.add("dve", lambda e: e.reciprocal(out=out.ap, in_=in_.ap), reads=[in_], writes=[out])

    def memset(self, eng, out, val):
        self.P.add(eng, lambda e: e.memset(out.ap, val), writes=[out])

    def dma_in(self, q, out, src_ap, semkey):
        self.P.add(q, lambda e: e.dma_start(out=out.ap, in_=src_ap), writes=[out], dma=True, semkey=semkey)

    def dma_out(self, q, dst_ap, in_, semkey):
        self.P.add(q, lambda e: e.dma_start(out=dst_ap, in_=in_.ap), reads=[in_], dma=True, semkey=semkey)

    def wq_push(self, src_ap, shape, cast=True):
        self.wq.append((src_ap, shape, cast))

    def wq_get(self, ahead=2):
        i = self.wq_next
        self.wq_next += 1
        while self.wq_issued < min(len(self.wq), i + 1 + ahead):
            j = self.wq_issued
            src, shape, cast = self.wq[j]
            slot = j % NWB
            b = self.sb(shape, BF16, OFF_WB + slot * WB_BYTES, "wb")
            if isinstance(src, list):
                for t, s_ in enumerate(src):
                    self.dma_in("pool" if cast else "sp", b[:, t], s_, "wb%d_%d" % (slot, t))
            else:
                self.dma_in("pool" if cast else "sp", b.all(), src, "wb%d" % slot)
            self.wq_bufs[j] = b
            self.wq_issued += 1
        return self.wq_bufs.pop(i)


def al(x, a=64):
    return (x + a - 1) // a * a


def chunked(ap, c0, c1):
    return ap[:, c0:c1].rearrange("(k p) n -> p k n", p=128)


def build(stop_after=None):
    B = Builder(stop_after)
    nc = B.nc
    P = B.P

    def din(name, shape, dt=F32):
        return nc.dram_tensor(name, list(shape), dt, kind="ExternalInput").ap()

    xT = din("xT", [BPC, D, S])
    ctxT = din("ctxT", [BPC, D, LC])
    cT = din("cT", [128, 8, 3])
    w_ada = din("w_ada", [2, D, 6 * D])
    b_ada = din("b_ada", [128, 2, 48])
    ng = din("ng", [128, 2, 2, 8])
    ev_w_in = din("ev_w_in", [D, 1536])
    ev_cw = din("ev_cw", [128, 4, 31])
    ev_ln = din("ev_ln", [128, 2, 4])
    ev_w_out = din("ev_w_out", [D, D])
    od_w_in = din("od_w_in", [D, 1536])
    od_qkg = din("od_qkg", [128, 2])
    od_bd = din("od_bd", [128, 2, 128])
    od_ps = din("od_ps", [128, 2])
    od_w_out = din("od_w_out", [D, D])
    w_up = din("w_up", [2, D, 2 * DFF])
    fcw = din("fcw", [128, 2, 44, 3])
    w_dn = din("w_dn", [2, DFF, D])
    dftL = din("dftL", [S, 2, S], BF16)
    dftC = din("dftC", [LC, 2, LC], BF16)
    cs128 = din("cs128", [128, 256], BF16)
    ropeT = din("ropeT", [128, 2, S], BF16)
    ropeR = din("ropeR", [128, 128])
    poolc = din("poolc", [128, 2, 17])
    outT = nc.dram_tensor("outT", [BPC, D, S], F32, kind="ExternalOutput").ap()

    st = ExitStack()
    with st:
        ps_h = st.enter_context(nc.psum_tensor("ps", [128, 4096], F32))
        PS = Buf("ps", ps_h, [128, 4096], "ps", 0, 4)

        X = B.sb([128, 8, S], F32, OFF_X, "X")
        C = B.sb([128, 8, LC], F32, OFF_C, "C")
        HTX = B.sb([128, 8, S], BF16, OFF_HTX, "HTX")
        HTC = B.sb([128, 8, LC], BF16, OFF_HTC, "HTC")
        CVX = B.sb([128, 4, S], F32, OFF_HTX, "CVX")
        CVC = B.sb([128, 4, LC], F32, OFF_HTC, "CVC")

        ones = B.const([128, 128], F32, "ones")
        B.memset("pool", ones.all(), 1.0)
        onesb = B.const([128, 128], BF16, "onesb")
        B.memset("pool", onesb.all(), 1.0)
        cTs = B.const([128, 8, 3], F32, "cTs")
        scT = B.const([128, 8, 3], BF16, "scT")
        bada = B.const([128, 2, 48], F32, "bada")
        ngs = B.const([128, 2, 2, 8], F32, "ngs")
        mod = B.const([128, 2, 48, 3], F32, "mod")
        gsm = B.const([128, 2, 2, 8, 3], F32, "gsm")
        cs = B.const([128, 256], BF16, "cs")
        evcw = B.const([128, 4, 31], F32, "evcw")
        evln = B.const([128, 2, 4], F32, "evln")
        fcws = B.const([128, 2, 44, 3], F32, "fcws")
        identf = B.const([128, 128], F32, "identf")

        B.dma_in("sp", cTs.all(), cT, "c0")
        B.dma_in("sp", bada.all(), b_ada, "c1")
        B.dma_in("sp", ngs.all(), ng, "c2")
        B.dma_in("sp", cs.all(), cs128, "c3")
        B.dma_in("sp", evcw.all(), ev_cw, "c4")
        B.dma_in("sp", evln.all(), ev_ln, "c5")
        B.dma_in("sp", fcws.all(), fcw, "c6")
        identd = din("identd", [128, 128])
        B.dma_in("sp", identf.all(), identd, "c7")

        B.act(scT.all(), cTs.all(), AF.Silu)
        def mod_spec_blk(i, cb):
            B.wq_push(chunked(w_ada[i], cb * 512, (cb + 1) * 512), [128, 8, 512])

        def mod_blk(i, cb):
            pc = B.psum(512)
            wb = B.wq_get()
            for mt in range(4):
                o = PS[:, pc + mt * 3: pc + mt * 3 + 3]
                for k in range(8):
                    B.mm(o, wb[:, k, mt * 128:(mt + 1) * 128], scT[:, k, :], k == 0, k == 7)
            for col in range(3):
                B.tt("dve", mod[:, i, cb * 4:(cb + 1) * 4, col], PS[:, pc + col: pc + 12: 3], bada[:, i, cb * 4:(cb + 1) * 4], ALU.add)

        def mod_fin_n(i, n_):
            sc0 = 8 if n_ == 0 else 32
            for col in range(3):
                B.stt("dve", gsm[:, i, n_, :, col], mod[:, i, sc0:sc0 + 8, col], 1.0, ngs[:, i, n_, :],
                      ALU.add, ALU.mult)

        def mod_fin(i):
            for n_ in range(2):
                mod_fin_n(i, n_)

        def mod_specs(i):
            for cb in range(12):
                mod_spec_blk(i, cb)

        def mod_compute(i):
            for cb in range(12):
                mod_blk(i, cb)
            mod_fin(i)

        mod_specs(0)
        for cb_ in range(4):
            mod_blk(0, cb_)
        mod_fin_n(0, 0)

        def modv(i, which, j, col):
            base = {"sh1": 0, "sc1": 8, "g1": 16, "sh2": 24, "sc2": 32, "g2": 40}[which]
            return mod[:, i, base + j, col:col + 1]

        def mkseq(L, resid, hT, cv, tab):
            s_ = Seq()
            s_.L, s_.resid, s_.hT, s_.cv, s_.tab = L, resid, hT, cv, tab
            s_.TB = min(512, L)
            s_.NB = L // s_.TB
            s_.LT = L // 128
            return s_

        SX = mkseq(S, X, HTX, CVX, dftL)
        SC = mkseq(LC, C, HTC, CVC, dftC)

        def norm_mod(sq_, i, n_, col):
            L, TB = sq_.L, sq_.TB
            sqt = [B.scr([128, 512], BF16, 0, "sqt0"), B.scr([128, 512], BF16, 2048, "sqt1")]
            rt = B.scr([128, 512], F32, 4096, "rt")
            tt_ = [B.scr([128, 512], F32, 6144, "tt0"), B.scr([128, 512], F32, 8192, "tt1")]
            shn = "sh1" if n_ == 0 else "sh2"
            for tb in range(sq_.NB):
                c0, c1 = tb * TB, (tb + 1) * TB
                pc = B.psum(TB)
                pv = PS[:, pc:pc + TB]
                for j in range(8):
                    sq = sqt[j % 2][:, 0:TB]
                    B.act(sq, sq_.resid[:, j, c0:c1], AF.Square)
                    B.mm(pv, onesb.all(), sq, j == 0, j == 7)
                r = rt[:, 0:TB]
                B.act(r, pv, AF.Ln, scale=1.0 / D, bias=EPS)
                B.act(r, r, AF.Exp, scale=-0.5)
                for j in range(8):
                    t = tt_[j % 2][:, 0:TB]
                    B.stt("dve", t, sq_.resid[:, j, c0:c1], gsm[:, i, n_, j, col:col + 1], r, ALU.mult, ALU.mult)
                    B.act(sq_.hT[:, j, c0:c1], t, AF.Identity, bias=modv(i, shn, j, col))

        def proj(sq_, pc, lhs_fn, nk, rhs_fn):
            TB = sq_.TB
            for tb in range(sq_.NB):
                o = PS[:, pc + tb * TB: pc + (tb + 1) * TB]
                for k in range(nk):
                    B.mm(o, lhs_fn(k), rhs_fn(k, tb), k == 0, k == nk - 1)

        def resid_add(sq_, m, pc, gate):
            TB = sq_.TB
            L = sq_.L
            B.stt("dve", sq_.resid[:, m, :], PS[:, pc:pc + L], gate, sq_.resid[:, m, :], ALU.mult, ALU.add)

        def even_specs(sq_):
            B.wq_push(chunked(ev_w_in, 0, 512), [128, 8, 512])
            L, TB = sq_.L, sq_.TB
            nlt = min(4, sq_.LT)
            for tb in range(sq_.NB):
                for g in range(sq_.LT // nlt):
                    src = [sq_.tab[(g * nlt + t) * 128:(g * nlt + t + 1) * 128, :, tb * TB:(tb + 1) * TB]
                           for t in range(nlt)]
                    B.wq_push(src, [128, nlt, 2, TB], cast=False)
            for cb in range(1, 3):
                B.wq_push(chunked(ev_w_in, cb * 512, (cb + 1) * 512), [128, 8, 512])
            for cb in range(2):
                B.wq_push(chunked(ev_w_out, cb * 512, (cb + 1) * 512), [128, 8, 512])

        def even_mixer(sq_, col):
            L, TB, NB, LT = sq_.L, sq_.TB, sq_.NB, sq_.LT
            hT = sq_.hT
            OFF_WIN = SCR_BYTES - 8 * L * 2
            win = B.scr([128, 8, L], BF16, OFF_WIN, "win")
            aT = B.scr([128, 4, L], BF16, OFF_WIN, "aT")
            Bt = B.scr([128, LT, 1024], BF16, 0, "Bt")
            wA = B.wq_get()
            for m in range(4):
                pc = B.psum(L)
                proj(sq_, pc, lambda k: wA[:, k, m * 128:(m + 1) * 128], 8, lambda k, tb: hT[:, k, tb * TB:(tb + 1) * TB])
                B.act(aT[:, m, :], PS[:, pc:pc + L], AF.Identity)
            B.ck("m1")
            for lt in range(LT):
                pc = B.psum(1024)
                for m in range(4):
                    B.mm(PS[:, pc + m * 256: pc + (m + 1) * 256], aT[:, m, lt * 128:(lt + 1) * 128], cs.all(), True, True)
                B.copy("dve", Bt[:, lt, :], PS[:, pc:pc + 1024])
            B.ck("m2")
            nlt = min(4, LT)
            for tb in range(NB):
                pcs = [B.psum(TB) for _ in range(4)]
                for g in range(LT // nlt):
                    tb_ = B.wq_get()
                    for t in range(nlt):
                        lt = g * nlt + t
                        for m in range(4):
                            o = PS[:, pcs[m]:pcs[m] + TB]
                            B.mm(o, Bt[:, lt, m * 256: m * 256 + 128], tb_[:, t, 0, :], lt == 0, False)
                            B.mm(o, Bt[:, lt, m * 256 + 128: m * 256 + 256], tb_[:, t, 1, :], False, lt == LT - 1)
                for m in range(4):
                    B.act(win[:, m, tb * TB:(tb + 1) * TB], PS[:, pcs[m]:pcs[m] + TB], AF.Identity)
            B.ck("m3")
            bpad = B.scr([128, 4, L + 30], BF16, 0, "bpad")
            sig = B.scr([128, L], F32, al(4 * (L + 30) * 2), "sig")
            B.memset("pool", bpad[:, :, 0:15], 0.0)
            B.memset("pool", bpad[:, :, L + 15:L + 30], 0.0)
            for j in range(4):
                if j % 2 == 0:
                    wb = B.wq_get()
                jj = j % 2
                pg = B.psum(L)
                proj(sq_, pg, lambda k: wb[:, k, jj * 256:jj * 256 + 128], 8, lambda k, tb: hT[:, k, tb * TB:(tb + 1) * TB])
                B.act(sig.all(), PS[:, pg:pg + L], AF.Sigmoid)
                pu = B.psum(L)
                proj(sq_, pu, lambda k: wb[:, k, jj * 256 + 128:jj * 256 + 256], 8, lambda k, tb: hT[:, k, tb * TB:(tb + 1) * TB])
                B.tt("dve", bpad[:, j, 15:15 + L], PS[:, pu:pu + L], sig.all(), ALU.mult)
            B.ck("m4")
            o5 = al(4 * (L + 30) * 2)
            diag = B.scr([128, 31, 128], BF16, o5, "diag")
            o5 += 31 * 128 * 2
            sq2 = [B.scr([128, 512], BF16, o5, "sq2a"), B.scr([128, 512], BF16, o5 + 2048, "sq2b")]
            o5 += 4096
            rs = B.scr([128, 512], F32, o5, "rs")
            mu = B.scr([128, 512], F32, o5 + 2048, "mu")
            o5 += 4096
            yt = [B.scr([128, 512], F32, o5, "yta")] * 2
            o5 += 2048
            assert o5 <= OFF_WIN
            cv = sq_.cv
            for j in range(4):
                for k in range(31):
                    if k % 3 == 2:
                        B.ts("dve", diag[:, k, :], identf.all(), evcw[:, j, k:k + 1], ALU.mult)
                    else:
                        B.act(diag[:, k, :], identf.all(), AF.Identity, scale=evcw[:, j, k:k + 1])
                pcs5 = [B.psum(TB) for _ in range(NB)]
                for k in range(31):
                    for tb in range(NB):
                        B.mm(PS[:, pcs5[tb]:pcs5[tb] + TB], diag[:, k, :], bpad[:, j, tb * TB + k: tb * TB + k + TB],
                             k == 0, k == 30)
                for tb in range(NB):
                    B.act(cv[:, j, tb * TB:(tb + 1) * TB], PS[:, pcs5[tb]:pcs5[tb] + TB], AF.Identity)
            for tb in range(NB):
                c0, c1 = tb * TB, (tb + 1) * TB
                pm = B.psum(TB)
                pe2 = B.psum(TB)
                for j in range(4):
                    B.mm(PS[:, pm:pm + TB], ones.all(), cv[:, j, c0:c1], j == 0, j == 3)
                for j in range(4):
                    sq = sq2[j % 2][:, 0:TB]
                    B.act(sq, cv[:, j, c0:c1], AF.Square)
                    B.mm(PS[:, pe2:pe2 + TB], onesb.all(), sq, j == 0, j == 3)
                m_ = mu[:, 0:TB]
                r_ = rs[:, 0:TB]
                B.act(m_, PS[:, pm:pm + TB], AF.Identity, scale=1.0 / 512)
                B.tt("dve", r_, m_, m_, ALU.mult)
                B.stt("dve", r_, PS[:, pe2:pe2 + TB], 1.0 / 512, r_, ALU.mult, ALU.subtract)
                B.ts("dve", r_, r_, 0.0, ALU.max)
                B.act(r_, r_, AF.Ln, scale=1.0, bias=EPS)
                B.act(r_, r_, AF.Exp, scale=-0.5)
                for j in range(4):
                    y = yt[j % 2][:, 0:TB]
                    B.tt("dve", y, cv[:, j, c0:c1], m_, ALU.subtract)
                    B.tt("dve", y, y, r_, ALU.mult)
                    B.act(win[:, 4 + j, c0:c1], y, AF.Silu, scale=evln[:, 0, j:j + 1], bias=evln[:, 1, j:j + 1])
            B.ck("m5")
            for m in range(8):
                if m % 4 == 0:
                    w_ = B.wq_get()
                mm_ = m % 4
                pc = B.psum(L)
                proj(sq_, pc, lambda k: w_[:, k, mm_ * 128:(mm_ + 1) * 128], 8, lambda k, tb: win[:, k, tb * TB:(tb + 1) * TB])
                B.ck("m7")
                resid_add(sq_, m, pc, modv(0, "g1", m, col))
                B.ck("m8")

        def ffn_specs(i, hook=None):
            nblk = 0
            for (g0, g1) in GROUPS:
                p0 = g0
                while p0 < g1:
                    n = min(2, g1 - p0)
                    B.wq_push(chunked(w_up[i], p0 * 256, (p0 + n) * 256), [128, 8, n * 256])
                    if hook is not None:
                        hook(nblk)
                    nblk += 1
                    p0 += n
                nk = g1 - g0
                for cb in range(4):
                    src = w_dn[i][g0 * 128:g1 * 128, cb * 256:(cb + 1) * 256].rearrange("(k p) n -> p k n", p=128)
                    B.wq_push(src, [128, nk, 256])

        def ffn(sq_, i, col, hook=None, cs_=None, ccol=None):
            L, TB, NB = sq_.L, sq_.TB, sq_.NB
            hT = sq_.hT
            nblk = 0
            PA, PB = 0, 2048
            a = B.scr([128, 8, L], BF16, 0, "a")
            o = 8 * L * 2
            ug = B.scr([128, L], F32, o, "ug")
            uv = B.scr([128, L], F32, o + 4 * L, "uv")
            sg = B.scr([128, L], BF16, o + 8 * L, "sg")
            o += 10 * L
            if cs_ is not None:
                Lc = cs_.L
                a_c = B.scr([128, 8, Lc], BF16, o, "a_c")
                o += 8 * Lc * 2
                ncb = 3
                ugc = [B.scr([128, Lc], F32, o + i_ * 10 * Lc, "ugc") for i_ in range(ncb)]
                uvc = [B.scr([128, Lc], F32, o + i_ * 10 * Lc + 4 * Lc, "uvc") for i_ in range(ncb)]
                sgc = [B.scr([128, Lc], BF16, o + i_ * 10 * Lc + 8 * Lc, "sgc") for i_ in range(ncb)]
                o += ncb * 10 * Lc
            assert o <= SCR_BYTES, (o, SCR_BYTES)

            def conv3(dst, pc, ch, n):
                w0 = fcws[:, i, ch, 0:1]
                w1 = fcws[:, i, ch, 1:2]
                w2 = fcws[:, i, ch, 2:3]
                B.act(dst.all(), PS[:, pc:pc + n], AF.Identity, scale=w1)
                B.stt("dve", dst[:, 1:n], PS[:, pc:pc + n - 1], w0, dst[:, 1:n], ALU.mult, ALU.add)
                B.stt("dve", dst[:, 0:n - 1], PS[:, pc + 1:pc + n], w2, dst[:, 0:n - 1], ALU.mult, ALU.add)

            for (g0, g1) in GROUPS:
                p0 = g0
                while p0 < g1:
                    n = min(2, g1 - p0)
                    wb = B.wq_get()
                    deferred = None
                    for q in range(n):
                        pr = p0 + q
                        proj(sq_, PA, lambda k: wb[:, k, q * 256: q * 256 + 128], 8, lambda k, tb: hT[:, k, tb * TB:(tb + 1) * TB])
                        conv3(ug, PA, 2 * pr, L)
                        B.act(sg.all(), ug.all(), AF.Silu)
                        proj(sq_, PB, lambda k: wb[:, k, q * 256 + 128: q * 256 + 256], 8, lambda k, tb: hT[:, k, tb * TB:(tb + 1) * TB])

                        def fin_v(pr=pr):
                            conv3(uv, PB, 2 * pr + 1, L)
                            B.tt("pool", a[:, pr - g0, :], sg.all(), uv.all(), ALU.mult)
                        if cs_ is not None and q == n - 1:
                            deferred = fin_v
                        else:
                            fin_v()
                    if cs_ is not None:
                        for q in range(n):
                            pr = p0 + q
                            bi = pr % ncb
                            pgc = PA + (2 * q) * 512
                            pvc = PA + (2 * q + 1) * 512
                            for k in range(8):
                                B.mm(PS[:, pgc:pgc + Lc], wb[:, k, q * 256: q * 256 + 128], cs_.hT[:, k, :], k == 0, k == 7)
                            for k in range(8):
                                B.mm(PS[:, pvc:pvc + Lc], wb[:, k, q * 256 + 128: q * 256 + 256], cs_.hT[:, k, :], k == 0, k == 7)
                            conv3(ugc[bi], pgc, 2 * pr, Lc)
                            B.act(sgc[bi].all(), ugc[bi].all(), AF.Silu)
                            conv3(uvc[bi], pvc, 2 * pr + 1, Lc)
                            B.tt("pool", a_c[:, pr - g0, :], sgc[bi].all(), uvc[bi].all(), ALU.mult)
                        deferred()
                    if hook is not None:
                        B.ps_ptr = 3
                        hook(nblk)
                    nblk += 1
                    p0 += n
                nk = g1 - g0
                for cb in range(4):
                    wd = B.wq_get()
                    for h in range(2):
                        m = cb * 2 + h
                        pc = PA if h == 0 else PB
                        proj(sq_, pc, lambda k: wd[:, k, h * 128:(h + 1) * 128], nk, lambda k, tb: a[:, k, tb * TB:(tb + 1) * TB])
                        resid_add(sq_, m, pc, modv(i, "g2", m, col))
                    if cs_ is not None:
                        for h in range(2):
                            m = cb * 2 + h
                            pc = PA + h * 512
                            for k in range(nk):
                                B.mm(PS[:, pc:pc + Lc], wd[:, k, h * 128:(h + 1) * 128], a_c[:, k, :], k == 0, k == nk - 1)
                            resid_add(cs_, m, pc, modv(i, "g2", m, ccol))
            B.ps_ptr = 0

        R0 = B.const([128, 128], F32, "R0")
        Rq = B.const([128, 128], F32, "Rq")
        Rk = B.const([128, 128], F32, "Rk")
        qkg = B.const([128, 2], F32, "qkg")
        bdm = B.const([128, 2, 128], BF16, "bdm")
        bdf = B.scr([128, 2, 128], F32, 0, "bdf")
        pscl = B.const([128, 2], F32, "pscl")
        plc = B.const([128, 2, 17], F32, "plc")
        bones = B.const([128, 128], F32, "bones")
        etmp = B.const([128, 8], F32, "etmp")
        B.dma_in("sp", R0.all(), ropeR, "d0")
        B.dma_in("sp", qkg.all(), od_qkg, "d1")
        B.dma_in("sp", bdf.all(), od_bd, "d2")
        B.dma_in("sp", pscl.all(), od_ps, "d3")
        B.dma_in("sp", plc.all(), poolc, "d4")
        B.copy("dve", bdm.all(), bdf.all())
        B.ts("dve", Rq.all(), R0.all(), qkg[:, 0:1], ALU.mult)
        B.ts("dve", Rk.all(), R0.all(), qkg[:, 1:2], ALU.mult)
        B.memset("pool", bones.all(), 0.0)
        B.memset("pool", bones[0:64, 0:64], 1.0)
        B.memset("pool", bones[64:128, 64:128], 1.0)

        def odd_specs():
            B.wq_push(chunked(od_w_in, 1024, 1536), [128, 8, 512])
            B.wq_push(chunked(od_w_in, 0, 512), [128, 8, 512])
            B.wq_push(chunked(od_w_in, 512, 1024), [128, 8, 512])
            for cb in range(2):
                B.wq_push(chunked(od_w_out, cb * 512, (cb + 1) * 512), [128, 8, 512])

        def odd_mixer(col):
            L, TB, NB = S, 512, 4
            NKT = 18
            o = 0
            poolout = B.scr([128, 2, S], BF16, o, "poolout"); o += 2 * S * 2
            Vp = B.scr([128, NKT, 4, 128], BF16, o, "Vp"); o += NKT * 4 * 128 * 2
            o_after_v = o
            q_sb = B.scr([128, 6, S], BF16, o, "q_sb"); o += 6 * S * 2
            k_sb = B.scr([128, 2, S + LC], BF16, o, "k_sb"); o += 2 * (S + LC) * 2
            rtab = B.scr([128, 2, S], BF16, o, "rtab")
            rtab_off = o
            o += 2 * S * 2
            assert o <= SCR_BYTES, (o, SCR_BYTES)
            attn = B.sb([128, 8, S], BF16, OFF_HTX, "attn")
            o2 = o_after_v
            upad = B.scr([128, S + 16], F32, o2, "upad"); o2 += al((S + 16) * 4)
            bA = B.scr([128, S + 16], F32, o2, "bA"); o2 += al((S + 16) * 4)
            bB = B.scr([128, S + 16], F32, o2, "bB"); o2 += al((S + 16) * 4)
            pooled = B.scr([128, 2, S], BF16, o2, "pooled"); o2 += 2 * S * 2
            assert o2 <= SCR_BYTES, (o2, SCR_BYTES)
            tmpc = [B.sb([128, 512], F32, OFF_C + i_ * 2048, "tmpc") for i_ in range(4)]
            hx = lambda k, tb: HTX[:, k, tb * TB:(tb + 1) * TB]

            w2 = B.wq_get()
            B.memset("pool", Vp.all(), 1.0)
            B.memset("pool", upad[:, 0:8], 0.0)
            B.memset("pool", upad[:, S + 8:S + 16], 0.0)
            W = S + 16
            for ch in range(2):
                pc = B.psum(L)
                proj(SX, pc, lambda k: w2[:, k, 256 + ch * 128: 256 + (ch + 1) * 128], 8, hx)
                B.act(upad[:, 8:8 + S], PS[:, pc:pc + L], AF.Identity)
                B.tt("dve", bA[:, 1:W], upad[:, 0:W - 1], upad[:, 1:W], ALU.add)
                B.tt("dve", bB[:, 2:W - 1], bA[:, 1:W - 2], bA[:, 3:W], ALU.add)
                if ch == 0:
                    srcs = [(0, bA), (64, bB)]
                else:
                    B.tt("dve", bA[:, 4:W - 3], bB[:, 2:W - 5], bB[:, 6:W - 1], ALU.add)
                    B.tt("dve", bB[64:128, 8:W - 8], bA[64:128, 4:W - 12], bA[64:128, 12:W - 4], ALU.add)
                    srcs = [(0, bA), (64, bB)]
                for (p0, src) in srcs:
                    ps_ = slice(p0, p0 + 64)
                    B.stt("dve", pooled[ps_, ch, :], src[ps_, 8:8 + S], plc[ps_, ch, 0:1], upad[ps_, 8:8 + S],
                          ALU.mult, ALU.subtract)
                    for (c0, e0) in ((0, 1), (S - 8, 9)):
                        B.tt("dve", etmp[ps_, :], src[ps_, 8 + c0:16 + c0], plc[ps_, ch, e0:e0 + 8], ALU.mult)
                        B.tt("dve", pooled[ps_, ch, c0:c0 + 8], etmp[ps_, :], upad[ps_, 8 + c0:16 + c0], ALU.subtract)
            for kt in range(NKT):
                pc = B.psum(256)
                for k in range(8):
                    lh = HTC[:, k, kt * 128:(kt + 1) * 128] if kt < 2 else HTX[:, k, (kt - 2) * 128:(kt - 1) * 128]
                    B.mm(PS[:, pc:pc + 256], lh, w2[:, k, 0:256], k == 0, k == 7)
                for h in range(4):
                    bs = 64 * (h % 2)
                    if True:
                        B.act(Vp[:, kt, h, bs:bs + 64], PS[:, pc + h * 64:pc + (h + 1) * 64], AF.Identity)
                    else:
                        B.copy("dve", Vp[:, kt, h, bs:bs + 64], PS[:, pc + h * 64:pc + (h + 1) * 64])
            for ch in range(2):
                pc = B.psum(L)
                proj(SX, pc, lambda k: bdm[:, ch, :], 1, lambda k, tb: pooled[:, ch, tb * TB:(tb + 1) * TB])
                B.act(poolout[:, ch, :], PS[:, pc:pc + L], AF.Identity, scale=pscl[:, ch:ch + 1])
            B.ck("o2")
            B.dma_in("sp", rtab.all(), ropeT, "rtab")

            rope_n = [0]

            def rope(pv, n, gcol, Rm, outv, tok0, use_rope):
                par = rope_n[0] % 2
                rope_n[0] += 1
                qs = tmpc[2 * par][:, 0:n]
                qq = tmpc[2 * par + 1][:, 0:n]
                B.act(qs, pv, AF.Identity)
                B.act(qq, pv, AF.Square)
                pm = B.psum(n)
                rt_ = PS[:, pm:pm + n]
                B.mm(rt_, bones.all(), qq, True, True)
                if use_rope:
                    pq = B.psum(n)
                    B.mm(PS[:, pq:pq + n], Rm.all(), qs, True, True)
                B.act(rt_, rt_, AF.Ln, scale=1.0 / 64, bias=EPS)
                B.act(rt_, rt_, AF.Exp, scale=-0.5)
                if use_rope:
                    B.tt("dve", qq, PS[:, pq:pq + n], rtab[:, 1, tok0:tok0 + n], ALU.mult)
                    B.stt("dve", qs, qs, qkg[:, gcol:gcol + 1], rtab[:, 0, tok0:tok0 + n], ALU.mult, ALU.mult)
                    B.tt("pool", qs, qs, qq, ALU.add)
                    B.tt("dve", outv, qs, rt_, ALU.mult)
                else:
                    B.stt("dve", outv, qs, qkg[:, gcol:gcol + 1], rt_, ALU.mult, ALU.mult)

            w0 = B.wq_get()
            for j in range(4):
                pc = B.psum(L)
                proj(SX, pc, lambda k: w0[:, k, j * 128:(j + 1) * 128], 8, hx)
                for tb in range(NB):
                    rope(PS[:, pc + tb * TB:pc + (tb + 1) * TB], TB, 0, Rq, q_sb[:, j, tb * TB:(tb + 1) * TB], tb * TB, True)
            w1 = B.wq_get()
            for j in range(4, 6):
                pc = B.psum(L)
                proj(SX, pc, lambda k: w1[:, k, (j - 4) * 128:(j - 3) * 128], 8, hx)
                for tb in range(NB):
                    rope(PS[:, pc + tb * TB:pc + (tb + 1) * TB], TB, 0, Rq, q_sb[:, j, tb * TB:(tb + 1) * TB], tb * TB, True)
            for c_ in range(2):
                pc = B.psum(L)
                proj(SX, pc, lambda k: w1[:, k, 256 + c_ * 128:256 + (c_ + 1) * 128], 8, hx)
                for tb in range(NB):
                    rope(PS[:, pc + tb * TB:pc + (tb + 1) * TB], TB, 1, Rk,
                         k_sb[:, c_, LC + tb * TB:LC + (tb + 1) * TB], tb * TB, True)
                pc = B.psum(LC)
                for k in range(8):
                    B.mm(PS[:, pc:pc + LC], w1[:, k, 256 + c_ * 128:256 + (c_ + 1) * 128], HTC[:, k, :], k == 0, k == 7)
                rope(PS[:, pc:pc + LC], LC, 1, Rk, k_sb[:, c_, 0:LC], 0, False)
            B.ck("o3")
            ACC = 0
            STS = [1024, 2048, 3072]
            accS = [B.sb([128, 1024], F32, OFF_C, "accS0"), B.sb([128, 1024], F32, OFF_C + 4096, "accS1")]
            pT3 = [B.scr([128, 1024], BF16, rtab_off + i_ * 2048, "pT3") for i_ in range(3)]
            dsh = B.scr([128, 512], F32, rtab_off + 6144, "dsh")
            its = []
            for j in range(6):
                for qb in range(4):
                    for kt in range(NKT):
                        its.append((j, qb, kt))
            kvs = [[QORDER[2 * j + hb] // 3 for hb in range(2)] for j in range(6)]
            for j in range(6):
                assert kvs[j][0] % 2 == 0 and kvs[j][1] % 2 == 1

            def qk(i):
                j, qb, kt = its[i]
                st_ = STS[i % 3]
                for hb in range(2):
                    pr = slice(64 * hb, 64 * hb + 64)
                    B.mm(PS[:, st_ + hb * 512: st_ + (hb + 1) * 512], k_sb[pr, kvs[j][hb] // 2, kt * 128:(kt + 1) * 128],
                         q_sb[pr, j, qb * 512:(qb + 1) * 512], True, True)

            qk(0)
            qk(1)
            for i, (j, qb, kt) in enumerate(its):
                if i + 2 < len(its):
                    qk(i + 2)
                st_ = STS[i % 3]
                pT = pT3[i % 3]
                B.act(pT.all(), PS[:, st_:st_ + 1024], AF.Exp, scale=0.125)
                for hb in range(2):
                    B.mm(PS[:, ACC + hb * 512:ACC + (hb + 1) * 512], Vp[:, kt, kvs[j][hb], :], pT[:, hb * 512:(hb + 1) * 512],
                         kt == 0, kt == NKT - 1)
                if kt == NKT - 1:
                    g = i // NKT
                    aS = accS[g % 2]
                    B.copy("dve", aS.all(), PS[:, ACC:ACC + 1024])
                    B.P.add("sp", (lambda aS=aS: (lambda e: e.dma_start(out=dsh[0:64, :].ap, in_=aS[64:128, 0:512].ap)))(),
                            reads=[aS[64:128, 0:512]], writes=[dsh[0:64, :]], dma=True, semkey="dshA")
                    B.P.add("sp", (lambda aS=aS: (lambda e: e.dma_start(out=dsh[64:128, :].ap, in_=aS[0:64, 512:1024].ap)))(),
                            reads=[aS[0:64, 512:1024]], writes=[dsh[64:128, :]], dma=True, semkey="dshB")
                    B.recip(dsh.all(), dsh.all())
                    B.tt("dve", attn[0:64, j, qb * 512:(qb + 1) * 512], aS[0:64, 0:512], dsh[0:64, :], ALU.mult)
                    B.tt("dve", attn[64:128, j, qb * 512:(qb + 1) * 512], aS[64:128, 512:1024], dsh[64:128, :], ALU.mult)
            B.ps_ptr = 0
            for m in range(8):
                if m % 4 == 0:
                    w_ = B.wq_get()
                mm_ = m % 4
                pc = B.psum(L)
                proj(SX, pc, lambda k: w_[:, k, mm_ * 128:(mm_ + 1) * 128], 8,
                     lambda k, tb: (attn[:, k, tb * TB:(tb + 1) * TB] if k < 6 else poolout[:, k - 6, tb * TB:(tb + 1) * TB]))
                resid_add(SX, m, pc, modv(1, "g1", m, col))

        try:
            for b in range(BPC):
                for j in range(8):
                    B.dma_in("sp", X[:, j, :], xT[b, j * 128:(j + 1) * 128, :], "xin%d" % j)
                B.dma_in("sp", C.all(), ctxT[b].rearrange("(j p) t -> p j t", p=128), "cin")
                B.ck("load")
                even_specs(SX)
                even_specs(SC)
                ffn_specs(0, (lambda n: mod_spec_blk(1, n)) if b == 0 else None)
                odd_specs()
                ffn_specs(1)
                norm_mod(SX, 0, 0, b)
                if b == 0:
                    for cb_ in range(4, 12):
                        mod_blk(0, cb_)
                    mod_fin_n(0, 1)
                B.ck("norm")
                even_mixer(SX, b)
                B.ck("mixer")
                norm_mod(SX, 0, 1, b)
                B.ck("ffn")
                norm_mod(SC, 0, 0, 2)
                even_mixer(SC, 2)
                norm_mod(SC, 0, 1, 2)
                ffn(SX, 0, b, (lambda n: mod_blk(1, n)) if b == 0 else None, SC, 2)
                if b == 0:
                    mod_fin(1)
                B.ck("l0")
                norm_mod(SX, 1, 0, b)
                norm_mod(SC, 1, 0, 2)
                odd_mixer(b)
                B.ck("mixer1")
                norm_mod(SX, 1, 1, b)
                ffn(SX, 1, b)
                for j in range(8):
                    B.dma_out("sp", outT[b, j * 128:(j + 1) * 128, :], X[:, j, :], "xout%d" % j)
        except _Stop:
            for j in range(8):
                B.dma_out("sp", outT[0, j * 128:(j + 1) * 128, :], X[:, j, :], "xout%d" % j)
            if B.stop_after == "norm":
                hdbg = nc.dram_tensor("hdbg", [128, 8, S], BF16, kind="ExternalOutput").ap()
                B.dma_out("sp", hdbg, HTX.all(), "hdbg")
            B.wq_next = len(B.wq)

        assert B.wq_next == len(B.wq), (B.wq_next, len(B.wq))
        P.finalize(st)
        P.emit()
    return nc


def _fm(v, nchunk):
    v = np.asarray(v, np.float32)
    lead = v.shape[:-1]
    v = v.reshape(lead + (nchunk, 128))
    return np.ascontiguousarray(np.moveaxis(v, -1, 0))


def _consts():
    bf = ml_dtypes.bfloat16
    out = {}
    for name, L in (("dftL", S), ("dftC", LC)):
        l = np.arange(L, dtype=np.int64)
        ang = 2.0 * np.pi * ((l[:, None] * l[None, :]) % L).astype(np.float64) / L
        t = np.stack([np.cos(ang), -np.sin(ang)], axis=1) / math.sqrt(L)
        out[name] = t.astype(np.float32).astype(bf)
    c = np.arange(128, dtype=np.int64)
    ang = 2.0 * np.pi * ((c[:, None] * c[None, :]) % 128).astype(np.float64) / 128
    out["cs128"] = (np.concatenate([np.cos(ang), np.sin(ang)], axis=1) / math.sqrt(128)).astype(np.float32).astype(bf)
    out["identd"] = np.eye(128, dtype=np.float32)
    t = np.arange(S)
    freqs = (10000.0 ** (-np.arange(16, dtype=np.float32) / 16)).astype(np.float32)
    rows = (t // 64).astype(np.float32)
    cols = (t % 64).astype(np.float32)
    rt = np.zeros((128, 2, S), np.float32)
    R = np.zeros((128, 128), np.float32)
    for p in range(128):
        dd = p % 64
        blk = dd // 32
        i = dd % 16
        half = (dd % 32) // 16
        pos = rows if blk == 0 else cols
        ang_ = (pos * freqs[i]).astype(np.float32)
        rt[p, 0] = np.cos(ang_)
        rt[p, 1] = np.sin(ang_)
        partner = p + 16 if half == 0 else p - 16
        R[partner, p] = -1.0 if half == 0 else 1.0
    out["ropeT"] = rt.astype(bf)
    out["ropeR"] = R
    pc = np.zeros((128, 2, 17), np.float32)
    for ch in range(2):
        for hp in range(2):
            w = (2, 4, 8, 16)[ch * 2 + hp]
            sl = slice(hp * 64, hp * 64 + 64)
            pc[sl, ch, 0] = 1.0 / w
            for e in range(8):
                tt_ = e
                lo = max(tt_ - w // 2, 0)
                hi = min(tt_ + w // 2 - 1, S - 1) + 1
                pc[sl, ch, 1 + e] = 1.0 / (hi - lo)
                tt_ = S - 8 + e
                lo = max(tt_ - w // 2, 0)
                hi = min(tt_ + w // 2 - 1, S - 1) + 1
                pc[sl, ch, 9 + e] = 1.0 / (hi - lo)
    out["poolc"] = pc
    return out


_NC_CACHE = {}
QORDER = [0, 3, 1, 4, 2, 5, 6, 9, 7, 10, 8, 11]


def _prep(inputs):
    f = lambda k: np.asarray(inputs[k], np.float32)
    x, c, ctx, c_ctx = f("x"), f("c"), f("ctx"), f("c_ctx")
    shared = {}
    shared["w_ada"] = np.ascontiguousarray(f("w_ada"))
    shared["b_ada"] = np.ascontiguousarray(np.transpose(f("b_ada").reshape(2, 48, 128), (2, 0, 1)))
    ngv = np.stack([f("norm1_g"), f("norm2_g")], axis=1)
    shared["ng"] = np.ascontiguousarray(np.transpose(ngv.reshape(2, 2, 8, 128), (3, 0, 1, 2)))
    w_in = f("ev_w_in")[0]
    cols = list(range(512))
    for j in range(4):
        cols += list(range(1024 + j * 128, 1024 + (j + 1) * 128))
        cols += list(range(512 + j * 128, 512 + (j + 1) * 128))
    shared["ev_w_in"] = np.ascontiguousarray(w_in[:, cols])
    shared["ev_cw"] = np.ascontiguousarray(np.transpose(f("ev_conv_w")[0].reshape(31, 4, 128), (2, 1, 0)))
    shared["ev_ln"] = np.ascontiguousarray(np.transpose(
        np.stack([f("ev_ln_g")[0], f("ev_ln_b")[0]], 0).reshape(2, 4, 128), (2, 0, 1)))
    shared["ev_w_out"] = np.ascontiguousarray(f("ev_w_out")[0])
    ow = f("od_w_in")[0]
    qcols = []
    for h in QORDER:
        qcols += list(range(h * 64, (h + 1) * 64))
    shared["od_w_in"] = np.ascontiguousarray(np.concatenate([ow[:, qcols], ow[:, 768:]], axis=1))
    shared["od_qkg"] = np.ascontiguousarray(np.stack([np.tile(f("od_q_g")[0], 2), np.tile(f("od_k_g")[0], 2)], axis=1))
    pw = f("od_pool_w")[0]
    bd = np.zeros((128, 2, 128), np.float32)
    for ch in range(2):
        bd[0:64, ch, 0:64] = pw[2 * ch]
        bd[64:128, ch, 64:128] = pw[2 * ch + 1]
    shared["od_bd"] = bd
    shared["od_ps"] = np.ascontiguousarray(f("od_pool_scale")[0].reshape(2, 128).T)
    wo = f("od_w_out")[0]
    shared["od_w_out"] = np.ascontiguousarray(np.concatenate([wo[qcols, :], wo[768:, :]], axis=0))
    wu = f("ffn_w_up")
    fc = f("ffn_conv_w")
    ucols = []
    for pr in range(NPAIR):
        ucols += list(range(pr * 128, (pr + 1) * 128))
        ucols += list(range(DFF + pr * 128, DFF + (pr + 1) * 128))
    shared["w_up"] = np.ascontiguousarray(wu[:, :, ucols])
    fcp = fc[:, :, ucols]
    shared["fcw"] = np.ascontiguousarray(np.transpose(fcp.reshape(2, 3, 44, 128), (3, 0, 2, 1)))
    shared["w_dn"] = np.ascontiguousarray(f("ffn_w_down"))
    shared.update(_consts())
    maps = []
    for core in range(NCORES):
        bs = [core * BPC + i for i in range(BPC)]
        m = dict(shared)
        m["xT"] = np.ascontiguousarray(np.transpose(x[bs], (0, 2, 1)))
        m["ctxT"] = np.ascontiguousarray(np.transpose(ctx[bs], (0, 2, 1)))
        cc = np.stack([c[bs[0]], c[bs[1]], c_ctx], axis=1)
        m["cT"] = np.ascontiguousarray(np.transpose(cc.reshape(8, 128, 3), (1, 0, 2)))
        maps.append(m)
    return maps


def kernel(**inputs):
    maps = _prep(inputs)
    if "nc" not in _NC_CACHE:
        _NC_CACHE["nc"] = build()
    nc = _NC_CACHE["nc"]
    res = run_bass_kernel_spmd(nc, maps, core_ids=list(range(NCORES)))
    outs = []
    for core in range(NCORES):
        o = np.asarray(res.results[core]["outT"], np.float32)
        outs.append(np.transpose(o, (0, 2, 1)))
    return np.ascontiguousarray(np.concatenate(outs, axis=0))
```

```python
import math
from contextlib import ExitStack

import numpy as np
import ml_dtypes

import concourse.bass as bass
import concourse.mybir as mybir
from concourse.bass_utils import run_bass_kernel_spmd

F32 = mybir.dt.float32
BF16 = mybir.dt.bfloat16
AF = mybir.ActivationFunctionType
ALU = mybir.AluOpType

D = 1024
S = 2048
LC = 256
NCORES = 8
BPC = 2
DFF = 2816
NPAIR = 22
EPS = 1e-6
GROUPS = [(0, 8), (8, 15), (15, 22)]


class Buf:
    def __init__(self, name, handle, shape, space="sb", base=0, esize=4):
        self.name, self.space, self.base, self.esize = name, space, base, esize
        self.h = handle
        self.shape = list(shape)
        st = [1] * len(shape)
        for i in range(len(shape) - 2, 0, -1):
            st[i] = st[i + 1] * shape[i + 1]
        self.strides = st

    def __getitem__(self, idx):
        if not isinstance(idx, tuple):
            idx = (idx,)
        idx = tuple(idx) + (slice(None),) * (len(self.shape) - len(idx))
        lo = hi = 0
        for d in range(1, len(self.shape)):
            i = idx[d]
            if isinstance(i, int):
                a, b = i, i
            else:
                a, b, s = i.indices(self.shape[d])
                n = max(0, (b - a + s - 1) // s)
                b = a + (n - 1) * s
            lo += a * self.strides[d]
            hi += b * self.strides[d]
        return View(self, self.h[idx], lo, hi + 1)

    def all(self):
        return self[tuple(slice(None) for _ in self.shape)]


class View:
    __slots__ = ("buf", "ap", "lo", "hi")

    def __init__(self, buf, ap, lo, hi):
        self.buf, self.ap, self.lo, self.hi = buf, ap, lo, hi

    @property
    def res(self):
        b = self.buf
        return (b.space, b.base + self.lo * b.esize, b.base + self.hi * b.esize)


class Op:
    __slots__ = ("eng", "fn", "deps", "signal", "dma", "semkey", "cnt", "waits")


class Prog:
    ENGS = ("pe", "act", "dve", "pool", "sp")

    def __init__(self, nc):
        self.nc = nc
        self.ops = []
        self.hist = {}

    def add(self, eng, fn, reads=(), writes=(), dma=False, semkey=None):
        idx = len(self.ops)
        reads = [r.res for r in reads if r is not None]
        writes = [w.res for w in writes if w is not None]
        deps = set()
        hist = self.hist
        for (name, lo, hi) in reads:
            for r in hist.get(name, ()):
                if r[3] and r[0] < hi and lo < r[1]:
                    deps.add(r[2])
        for (name, lo, hi) in writes:
            for r in hist.get(name, ()):
                if r[0] < hi and lo < r[1]:
                    deps.add(r[2])
        for (name, lo, hi) in writes:
            lst = hist.setdefault(name, [])
            lst[:] = [r for r in lst if not (lo <= r[0] and r[1] <= hi)]
            lst.append((lo, hi, idx, True, eng, dma))
        for (name, lo, hi) in reads:
            lst = hist.setdefault(name, [])
            if not dma:
                lst[:] = [r for r in lst if not ((not r[3]) and r[4] == eng and (not r[5])
                                                 and lo <= r[0] and r[1] <= hi)]
            lst.append((lo, hi, idx, False, eng, dma))
        op = Op()
        op.eng, op.fn, op.dma, op.semkey = eng, fn, dma, semkey
        op.deps, op.signal, op.cnt, op.waits = deps, dma, 0, None
        self.ops.append(op)
        return idx

    def finalize(self, stack):
        nc, ops = self.nc, self.ops

        def skip(p, op):
            return (not p.dma) and (not op.dma) and p.eng == "pe" and op.eng == "pe"

        for op in ops:
            for d in op.deps:
                p = ops[d]
                if not p.dma and not skip(p, op):
                    p.signal = True
        sems = {}

        def sem(key):
            if key not in sems:
                sems[key] = stack.enter_context(nc.semaphore("s%d" % len(sems)))
            return sems[key]

        cnt = {}
        for op in ops:
            key = ("d", op.semkey) if op.dma else ("e", op.eng)
            if op.signal:
                cnt[key] = cnt.get(key, 0) + (16 if op.dma else 1)
                op.cnt = cnt[key]
        known = {e: {} for e in self.ENGS}
        for op in ops:
            w = {}
            for d in op.deps:
                p = ops[d]
                if skip(p, op):
                    continue
                key = ("d", p.semkey) if p.dma else ("e", p.eng)
                if p.cnt > w.get(key, 0):
                    w[key] = p.cnt
            kn = known[op.eng]
            out = []
            for key, v in w.items():
                if kn.get(key, 0) >= v:
                    continue
                kn[key] = v
                out.append((sem(key), v))
            op.waits = out
            if op.signal:
                sem(("d", op.semkey) if op.dma else ("e", op.eng))
        self.sems = sems

    def emit(self):
        nc = self.nc
        by = {e: [op for op in self.ops if op.eng == e] for e in self.ENGS}
        sems = self.sems

        def run(e, lst):
            for op in lst:
                for (s, v) in op.waits:
                    e.wait_ge(s, v)
                ins = op.fn(e)
                if op.signal:
                    key = ("d", op.semkey) if op.dma else ("e", op.eng)
                    ins.then_inc(sems[key], 16 if op.dma else 1)

        with nc.Block() as block:
            @block.tensor
            def _(e):
                run(e, by["pe"])

            @block.scalar
            def _(e):
                run(e, by["act"])

            @block.vector
            def _(e):
                run(e, by["dve"])

            @block.gpsimd
            def _(e):
                run(e, by["pool"])

            @block.sync
            def _(e):
                run(e, by["sp"])
                fin = {}
                for op in self.ops:
                    if op.dma:
                        fin[("d", op.semkey)] = op.cnt
                for key, v in fin.items():
                    e.wait_ge(sems[key], v)


SB_LO = 16384
SB_HI = 229376 - 256

OFF_X = SB_LO
OFF_C = OFF_X + 8 * S * 4
OFF_HTX = OFF_C + 8 * LC * 4
OFF_HTC = OFF_HTX + 8 * S * 2
OFF_WB = OFF_HTC + 8 * LC * 2
NWB = 3
WB_BYTES = 8192
OFF_CONST = OFF_WB + NWB * WB_BYTES
CONST_BYTES = 8704
OFF_SCR = OFF_CONST + CONST_BYTES
SCR_BYTES = SB_HI - OFF_SCR


class Seq:
    pass


class _Stop(Exception):
    pass


class Builder:
    def __init__(self, stop_after=None):
        self.stop_after = stop_after
        self.nc = bass.Bass("TRN2", target_bir_lowering=False)
        self.P = Prog(self.nc)
        self.uid = 0
        self.ps_ptr = 0
        self.wb_ptr = 0
        self.const_ptr = 0
        self.wq = []
        self.wq_issued = 0
        self.wq_next = 0
        self.wq_bufs = {}

    def ck(self, name):
        if self.stop_after == name:
            raise _Stop()

    def sb(self, shape, dt, off, name=None):
        self.uid += 1
        name = "%s_%d" % (name or "t", self.uid)
        es = 4 if dt == F32 else 2
        n = 1
        for s_ in shape[1:]:
            n *= s_
        assert off >= SB_LO and off + n * es <= SB_HI, (name, off, n * es)
        h = self.nc.alloc_sbuf_tensor_at(name, list(shape), dt, offset=off)
        return Buf(name, h, shape, "sb", off, es)

    def scr(self, shape, dt, off, name=None):
        es = 4 if dt == F32 else 2
        n = 1
        for s_ in shape[1:]:
            n *= s_
        assert off + n * es <= SCR_BYTES, (name, off, n * es, SCR_BYTES)
        return self.sb(shape, dt, OFF_SCR + off, name)

    def const(self, shape, dt, name=None):
        es = 4 if dt == F32 else 2
        n = 1
        for s_ in shape[1:]:
            n *= s_
        off = (self.const_ptr + 31) // 32 * 32
        assert off + n * es <= CONST_BYTES, ("const overflow", name)
        self.const_ptr = off + n * es
        return self.sb(shape, dt, OFF_CONST + off, name)

    def psum(self, ncols):
        nb = (ncols + 511) // 512
        nb = {1: 1, 2: 2, 3: 4, 4: 4}[nb]
        p = (self.ps_ptr + nb - 1) // nb * nb
        if p + nb > 8:
            p = 0
        self.ps_ptr = (p + nb) % 8
        return p * 512

    def mm(self, out, lhsT, rhs, start, stop):
        self.P.add("pe", lambda e: e.matmul(out.ap, lhsT=lhsT.ap, rhs=rhs.ap, start=start, stop=stop),
                   reads=[lhsT, rhs], writes=[out])

    def act(self, out, in_, func, scale=1.0, bias=None, eng="act"):
        rd = [in_]
        kw = {}
        if isinstance(scale, View):
            rd.append(scale)
            kw["scale"] = scale.ap
        else:
            kw["scale"] = float(scale)
        if isinstance(bias, View):
            rd.append(bias)
            kw["bias"] = bias.ap
        elif bias is not None:
            kw["bias"] = float(bias)
        self.P.add("act", lambda e: e.activation(out=out.ap, in_=in_.ap, func=func, **kw),
                   reads=rd, writes=[out])

    def tt(self, eng, out, in0, in1, op):
        self.P.add(eng, lambda e: e.tensor_tensor(out=out.ap, in0=in0.ap, in1=in1.ap, op=op),
                   reads=[in0, in1], writes=[out])

    def ts(self, eng, out, in0, s1, op0, s2=None, op1=None):
        rd = [in0]
        a1 = s1.ap if isinstance(s1, View) else float(s1)
        if isinstance(s1, View):
            rd.append(s1)
        a2 = None
        if s2 is not None:
            a2 = s2.ap if isinstance(s2, View) else float(s2)
            if isinstance(s2, View):
                rd.append(s2)
        if op1 is None:
            self.P.add(eng, lambda e: e.tensor_scalar(out=out.ap, in0=in0.ap, scalar1=a1, scalar2=None, op0=op0),
                       reads=rd, writes=[out])
        else:
            self.P.add(eng, lambda e: e.tensor_scalar(out=out.ap, in0=in0.ap, scalar1=a1, scalar2=a2, op0=op0, op1=op1),
                       reads=rd, writes=[out])

    def stt(self, eng, out, in0, scalar, in1, op0, op1):
        rd = [in0, in1]
        a = scalar.ap if isinstance(scalar, View) else float(scalar)
        if isinstance(scalar, View):
            rd.append(scalar)
        self.P.add(eng, lambda e: e.scalar_tensor_tensor(out=out.ap, in0=in0.ap, scalar=a, in1=in1.ap, op0=op0, op1=op1),
                   reads=rd, writes=[out])

    def copy(self, eng, out, in_):
        self.P.add(eng, lambda e: e.tensor_copy(out=out.ap, in_=in_.ap), reads=[in_], writes=[out])

    def recip(self, out, in_):
        self.P.add("dve", lambda e: e.reciprocal(out=out.ap, in_=in_.ap), reads=[in_], writes=[out])

    def memset(self, eng, out, val):
        self.P.add(eng, lambda e: e.memset(out.ap, val), writes=[out])

    def dma_in(self, q, out, src_ap, semkey):
        self.P.add(q, lambda e: e.dma_start(out=out.ap, in_=src_ap), writes=[out], dma=True, semkey=semkey)

    def dma_out(self, q, dst_ap, in_, semkey):
        self.P.add(q, lambda e: e.dma_start(out=dst_ap, in_=in_.ap), reads=[in_], dma=True, semkey=semkey)

    def wq_push(self, src_ap, shape, cast=True):
        self.wq.append((src_ap, shape, cast))

    def wq_get(self, ahead=2):
        i = self.wq_next
        self.wq_next += 1
        while self.wq_issued < min(len(self.wq), i + 1 + ahead):
            j = self.wq_issued
            src, shape, cast = self.wq[j]
            slot = j % NWB
            b = self.sb(shape, BF16, OFF_WB + slot * WB_BYTES, "wb")
            if isinstance(src, list):
                for t, s_ in enumerate(src):
                    self.dma_in("pool" if cast else "sp", b[:, t], s_, "wb%d_%d" % (slot, t))
            else:
                self.dma_in("pool" if cast else "sp", b.all(), src, "wb%d" % slot)
            self.wq_bufs[j] = b
            self.wq_issued += 1
        return self.wq_bufs.pop(i)


def al(x, a=64):
    return (x + a - 1) // a * a


def chunked(ap, c0, c1):
    return ap[:, c0:c1].rearrange("(k p) n -> p k n", p=128)


def build(stop_after=None):
    B = Builder(stop_after)
    nc = B.nc
    P = B.P

    def din(name, shape, dt=F32):
        return nc.dram_tensor(name, list(shape), dt, kind="ExternalInput").ap()

    xT = din("xT", [BPC, D, S])
    ctxT = din("ctxT", [BPC, D, LC])
    cT = din("cT", [128, 8, 3])
    w_ada = din("w_ada", [2, D, 6 * D])
    b_ada = din("b_ada", [128, 2, 48])
    ng = din("ng", [128, 2, 2, 8])
    ev_w_in = din("ev_w_in", [D, 1536])
    ev_cw = din("ev_cw", [128, 4, 31])
    ev_ln = din("ev_ln", [128, 2, 4])
    ev_w_out = din("ev_w_out", [D, D])
    od_w_in = din("od_w_in", [D, 1536])
    od_qkg = din("od_qkg", [128, 2])
    od_bd = din("od_bd", [128, 2, 128])
    od_ps = din("od_ps", [128, 2])
    od_w_out = din("od_w_out", [D, D])
    w_up = din("w_up", [2, D, 2 * DFF])
    fcw = din("fcw", [128, 2, 44, 3])
    w_dn = din("w_dn", [2, DFF, D])
    dftL = din("dftL", [S, 2, S], BF16)
    dftC = din("dftC", [LC, 2, LC], BF16)
    cs128 = din("cs128", [128, 256], BF16)
    ropeT = din("ropeT", [128, 2, S], BF16)
    ropeR = din("ropeR", [128, 128])
    poolc = din("poolc", [128, 2, 17])
    outT = nc.dram_tensor("outT", [BPC, D, S], F32, kind="ExternalOutput").ap()

    st = ExitStack()
    with st:
        ps_h = st.enter_context(nc.psum_tensor("ps", [128, 4096], F32))
        PS = Buf("ps", ps_h, [128, 4096], "ps", 0, 4)

        X = B.sb([128, 8, S], F32, OFF_X, "X")
        C = B.sb([128, 8, LC], F32, OFF_C, "C")
        HTX = B.sb([128, 8, S], BF16, OFF_HTX, "HTX")
        HTC = B.sb([128, 8, LC], BF16, OFF_HTC, "HTC")
        CVX = B.sb([128, 4, S], F32, OFF_HTX, "CVX")
        CVC = B.sb([128, 4, LC], F32, OFF_HTC, "CVC")

        ones = B.const([128, 128], F32, "ones")
        B.memset("pool", ones.all(), 1.0)
        onesb = B.const([128, 128], BF16, "onesb")
        B.memset("pool", onesb.all(), 1.0)
        cTs = B.const([128, 8, 3], F32, "cTs")
        scT = B.const([128, 8, 3], BF16, "scT")
        bada = B.const([128, 2, 48], F32, "bada")
        ngs = B.const([128, 2, 2, 8], F32, "ngs")
        mod = B.const([128, 2, 48, 3], F32, "mod")
        gsm = B.const([128, 2, 2, 8, 3], F32, "gsm")
        cs = B.const([128, 256], BF16, "cs")
        evcw = B.const([128, 4, 31], F32, "evcw")
        evln = B.const([128, 2, 4], F32, "evln")
        fcws = B.const([128, 2, 44, 3], F32, "fcws")
        identf = B.const([128, 128], F32, "identf")

        B.dma_in("sp", cTs.all(), cT, "c0")
        B.dma_in("sp", bada.all(), b_ada, "c1")
        B.dma_in("sp", ngs.all(), ng, "c2")
        B.dma_in("sp", cs.all(), cs128, "c3")
        B.dma_in("sp", evcw.all(), ev_cw, "c4")
        B.dma_in("sp", evln.all(), ev_ln, "c5")
        B.dma_in("sp", fcws.all(), fcw, "c6")
        identd = din("identd", [128, 128])
        B.dma_in("sp", identf.all(), identd, "c7")

        B.act(scT.all(), cTs.all(), AF.Silu)
        def mod_spec_blk(i, cb):
            B.wq_push(chunked(w_ada[i], cb * 512, (cb + 1) * 512), [128, 8, 512])

        def mod_blk(i, cb):
            pc = B.psum(512)
            wb = B.wq_get()
            for mt in range(4):
                o = PS[:, pc + mt * 3: pc + mt * 3 + 3]
                for k in range(8):
                    B.mm(o, wb[:, k, mt * 128:(mt + 1) * 128], scT[:, k, :], k == 0, k == 7)
            for col in range(3):
                B.tt("dve", mod[:, i, cb * 4:(cb + 1) * 4, col], PS[:, pc + col: pc + 12: 3], bada[:, i, cb * 4:(cb + 1) * 4], ALU.add)

        def mod_fin_n(i, n_):
            sc0 = 8 if n_ == 0 else 32
            for col in range(3):
                B.stt("dve", gsm[:, i, n_, :, col], mod[:, i, sc0:sc0 + 8, col], 1.0, ngs[:, i, n_, :],
                      ALU.add, ALU.mult)

        def mod_fin(i):
            for n_ in range(2):
                mod_fin_n(i, n_)

        def mod_specs(i):
            for cb in range(12):
                mod_spec_blk(i, cb)

        def mod_compute(i):
            for cb in range(12):
                mod_blk(i, cb)
            mod_fin(i)

        mod_specs(0)
        for cb_ in range(4):
            mod_blk(0, cb_)
        mod_fin_n(0, 0)

        def modv(i, which, j, col):
            base = {"sh1": 0, "sc1": 8, "g1": 16, "sh2": 24, "sc2": 32, "g2": 40}[which]
            return mod[:, i, base + j, col:col + 1]

        def mkseq(L, resid, hT, cv, tab):
            s_ = Seq()
            s_.L, s_.resid, s_.hT, s_.cv, s_.tab = L, resid, hT, cv, tab
            s_.TB = min(512, L)
            s_.NB = L // s_.TB
            s_.LT = L // 128
            return s_

        SX = mkseq(S, X, HTX, CVX, dftL)
        SC = mkseq(LC, C, HTC, CVC, dftC)

        def norm_mod(sq_, i, n_, col):
            L, TB = sq_.L, sq_.TB
            sqb = [B.scr([128, L], BF16, 0, "sqb0"), B.scr([128, L], BF16, 2 * L, "sqb1")]
            tb_ = [B.scr([128, L], F32, 4 * L, "nt0"), B.scr([128, L], F32, 8 * L, "nt1")]
            shn = "sh1" if n_ == 0 else "sh2"
            pc = B.psum(L)
            pv = PS[:, pc:pc + L]
            for j in range(8):
                sq = sqb[j % 2]
                if j % 2 == 0:
                    B.act(sq.all(), sq_.resid[:, j, :], AF.Square)
                else:
                    B.tt("dve", sq.all(), sq_.resid[:, j, :], sq_.resid[:, j, :], ALU.mult)
                for t in range(sq_.NB):
                    B.mm(PS[:, pc + t * TB:pc + (t + 1) * TB], onesb.all(), sq[:, t * TB:(t + 1) * TB], j == 0, j == 7)
            B.act(pv, pv, AF.Ln, scale=1.0 / D, bias=EPS)
            B.act(pv, pv, AF.Exp, scale=-0.5)
            for j in range(8):
                t = tb_[j % 2]
                B.stt("dve", t.all(), sq_.resid[:, j, :], gsm[:, i, n_, j, col:col + 1], pv, ALU.mult, ALU.mult)
                B.act(sq_.hT[:, j, :], t.all(), AF.Identity, bias=modv(i, shn, j, col))

        def proj(sq_, pc, lhs_fn, nk, rhs_fn):
            TB = sq_.TB
            for tb in range(sq_.NB):
                o = PS[:, pc + tb * TB: pc + (tb + 1) * TB]
                for k in range(nk):
                    B.mm(o, lhs_fn(k), rhs_fn(k, tb), k == 0, k == nk - 1)

        def resid_add(sq_, m, pc, gate):
            TB = sq_.TB
            L = sq_.L
            B.stt("dve", sq_.resid[:, m, :], PS[:, pc:pc + L], gate, sq_.resid[:, m, :], ALU.mult, ALU.add)

        def even_specs(sq_):
            B.wq_push(chunked(ev_w_in, 0, 512), [128, 8, 512])
            L, TB = sq_.L, sq_.TB
            nlt = min(4, sq_.LT)
            for tb in range(sq_.NB):
                for g in range(sq_.LT // nlt):
                    src = [sq_.tab[(g * nlt + t) * 128:(g * nlt + t + 1) * 128, :, tb * TB:(tb + 1) * TB]
                           for t in range(nlt)]
                    B.wq_push(src, [128, nlt, 2, TB], cast=False)
            for cb in range(1, 3):
                B.wq_push(chunked(ev_w_in, cb * 512, (cb + 1) * 512), [128, 8, 512])
            for cb in range(2):
                B.wq_push(chunked(ev_w_out, cb * 512, (cb + 1) * 512), [128, 8, 512])

        def even_mixer(sq_, col):
            L, TB, NB, LT = sq_.L, sq_.TB, sq_.NB, sq_.LT
            hT = sq_.hT
            OFF_WIN = SCR_BYTES - 8 * L * 2
            win = B.scr([128, 8, L], BF16, OFF_WIN, "win")
            aT = B.scr([128, 4, L], BF16, OFF_WIN, "aT")
            Bt = B.scr([128, LT, 1024], BF16, 0, "Bt")
            wA = B.wq_get()
            for m in range(4):
                pc = B.psum(L)
                proj(sq_, pc, lambda k: wA[:, k, m * 128:(m + 1) * 128], 8, lambda k, tb: hT[:, k, tb * TB:(tb + 1) * TB])
                B.act(aT[:, m, :], PS[:, pc:pc + L], AF.Identity)
            B.ck("m1")
            for lt in range(LT):
                pc = B.psum(1024)
                for m in range(4):
                    B.mm(PS[:, pc + m * 256: pc + (m + 1) * 256], aT[:, m, lt * 128:(lt + 1) * 128], cs.all(), True, True)
                B.copy("dve", Bt[:, lt, :], PS[:, pc:pc + 1024])
            B.ck("m2")
            nlt = min(4, LT)
            for tb in range(NB):
                pcs = [B.psum(TB) for _ in range(4)]
                for g in range(LT // nlt):
                    tb_ = B.wq_get()
                    for t in range(nlt):
                        lt = g * nlt + t
                        for m in range(4):
                            o = PS[:, pcs[m]:pcs[m] + TB]
                            B.mm(o, Bt[:, lt, m * 256: m * 256 + 128], tb_[:, t, 0, :], lt == 0, False)
                            B.mm(o, Bt[:, lt, m * 256 + 128: m * 256 + 256], tb_[:, t, 1, :], False, lt == LT - 1)
                for m in range(4):
                    B.act(win[:, m, tb * TB:(tb + 1) * TB], PS[:, pcs[m]:pcs[m] + TB], AF.Identity)
            B.ck("m3")
            bpad = B.scr([128, 4, L + 30], BF16, 0, "bpad")
            sig = B.scr([128, L], F32, al(4 * (L + 30) * 2), "sig")
            B.memset("pool", bpad[:, :, 0:15], 0.0)
            B.memset("pool", bpad[:, :, L + 15:L + 30], 0.0)
            for j in range(4):
                if j % 2 == 0:
                    wb = B.wq_get()
                jj = j % 2
                pg = B.psum(L)
                proj(sq_, pg, lambda k: wb[:, k, jj * 256:jj * 256 + 128], 8, lambda k, tb: hT[:, k, tb * TB:(tb + 1) * TB])
                B.act(sig.all(), PS[:, pg:pg + L], AF.Sigmoid)
                pu = B.psum(L)
                proj(sq_, pu, lambda k: wb[:, k, jj * 256 + 128:jj * 256 + 256], 8, lambda k, tb: hT[:, k, tb * TB:(tb + 1) * TB])
                B.tt("dve", bpad[:, j, 15:15 + L], PS[:, pu:pu + L], sig.all(), ALU.mult)
            B.ck("m4")
            o5 = al(4 * (L + 30) * 2)
            diag = B.scr([128, 31, 128], BF16, o5, "diag")
            o5 += 31 * 128 * 2
            sq2 = [B.scr([128, 512], BF16, o5, "sq2a"), B.scr([128, 512], BF16, o5 + 2048, "sq2b")]
            o5 += 4096
            rs = B.scr([128, 512], F32, o5, "rs")
            mu = B.scr([128, 512], F32, o5 + 2048, "mu")
            o5 += 4096
            yt = [B.scr([128, 512], F32, o5, "yta")] * 2
            o5 += 2048
            assert o5 <= OFF_WIN
            cv = sq_.cv
            for j in range(4):
                for k in range(31):
                    if k % 3 == 2:
                        B.ts("dve", diag[:, k, :], identf.all(), evcw[:, j, k:k + 1], ALU.mult)
                    else:
                        B.act(diag[:, k, :], identf.all(), AF.Identity, scale=evcw[:, j, k:k + 1])
                pcs5 = [B.psum(TB) for _ in range(NB)]
                for k in range(31):
                    for tb in range(NB):
                        B.mm(PS[:, pcs5[tb]:pcs5[tb] + TB], diag[:, k, :], bpad[:, j, tb * TB + k: tb * TB + k + TB],
                             k == 0, k == 30)
                for tb in range(NB):
                    B.act(cv[:, j, tb * TB:(tb + 1) * TB], PS[:, pcs5[tb]:pcs5[tb] + TB], AF.Identity)
            for tb in range(NB):
                c0, c1 = tb * TB, (tb + 1) * TB
                pm = B.psum(TB)
                pe2 = B.psum(TB)
                for j in range(4):
                    B.mm(PS[:, pm:pm + TB], ones.all(), cv[:, j, c0:c1], j == 0, j == 3)
                for j in range(4):
                    sq = sq2[j % 2][:, 0:TB]
                    B.act(sq, cv[:, j, c0:c1], AF.Square)
                    B.mm(PS[:, pe2:pe2 + TB], onesb.all(), sq, j == 0, j == 3)
                m_ = mu[:, 0:TB]
                r_ = rs[:, 0:TB]
                B.act(m_, PS[:, pm:pm + TB], AF.Identity, scale=1.0 / 512)
                B.tt("dve", r_, m_, m_, ALU.mult)
                B.stt("dve", r_, PS[:, pe2:pe2 + TB], 1.0 / 512, r_, ALU.mult, ALU.subtract)
                B.ts("dve", r_, r_, 0.0, ALU.max)
                B.act(r_, r_, AF.Ln, scale=1.0, bias=EPS)
                B.act(r_, r_, AF.Exp, scale=-0.5)
                for j in range(4):
                    y = yt[j % 2][:, 0:TB]
                    B.tt("dve", y, cv[:, j, c0:c1], m_, ALU.subtract)
                    B.tt("dve", y, y, r_, ALU.mult)
                    B.act(win[:, 4 + j, c0:c1], y, AF.Silu, scale=evln[:, 0, j:j + 1], bias=evln[:, 1, j:j + 1])
            B.ck("m5")
            for m in range(8):
                if m % 4 == 0:
                    w_ = B.wq_get()
                mm_ = m % 4
                pc = B.psum(L)
                proj(sq_, pc, lambda k: w_[:, k, mm_ * 128:(mm_ + 1) * 128], 8, lambda k, tb: win[:, k, tb * TB:(tb + 1) * TB])
                B.ck("m7")
                resid_add(sq_, m, pc, modv(0, "g1", m, col))
                B.ck("m8")

        def ffn_specs(i, hook=None):
            nblk = 0
            for (g0, g1) in GROUPS:
                p0 = g0
                while p0 < g1:
                    n = min(2, g1 - p0)
                    B.wq_push(chunked(w_up[i], p0 * 256, (p0 + n) * 256), [128, 8, n * 256])
                    if hook is not None:
                        hook(nblk)
                    nblk += 1
                    p0 += n
                nk = g1 - g0
                for cb in range(4):
                    src = w_dn[i][g0 * 128:g1 * 128, cb * 256:(cb + 1) * 256].rearrange("(k p) n -> p k n", p=128)
                    B.wq_push(src, [128, nk, 256])

        def ffn(sq_, i, col, hook=None, cs_=None, ccol=None):
            L, TB, NB = sq_.L, sq_.TB, sq_.NB
            hT = sq_.hT
            nblk = 0
            PA, PB = 0, 2048
            a = B.scr([128, 8, L], BF16, 0, "a")
            o = 8 * L * 2
            ug = B.scr([128, L], F32, o, "ug")
            uv = B.scr([128, L], F32, o + 4 * L, "uv")
            sg = B.scr([128, L], BF16, o + 8 * L, "sg")
            o += 10 * L
            if cs_ is not None:
                Lc = cs_.L
                a_c = B.scr([128, 8, Lc], BF16, o, "a_c")
                o += 8 * Lc * 2
                ncb = 3
                ugc = [B.scr([128, Lc], F32, o + i_ * 10 * Lc, "ugc") for i_ in range(ncb)]
                uvc = [B.scr([128, Lc], F32, o + i_ * 10 * Lc + 4 * Lc, "uvc") for i_ in range(ncb)]
                sgc = [B.scr([128, Lc], BF16, o + i_ * 10 * Lc + 8 * Lc, "sgc") for i_ in range(ncb)]
                o += ncb * 10 * Lc
            assert o <= SCR_BYTES, (o, SCR_BYTES)

            def conv3(dst, pc, ch, n):
                w0 = fcws[:, i, ch, 0:1]
                w1 = fcws[:, i, ch, 1:2]
                w2 = fcws[:, i, ch, 2:3]
                B.act(dst.all(), PS[:, pc:pc + n], AF.Identity, scale=w1)
                B.stt("dve", dst[:, 1:n], PS[:, pc:pc + n - 1], w0, dst[:, 1:n], ALU.mult, ALU.add)
                B.stt("dve", dst[:, 0:n - 1], PS[:, pc + 1:pc + n], w2, dst[:, 0:n - 1], ALU.mult, ALU.add)

            for (g0, g1) in GROUPS:
                p0 = g0
                while p0 < g1:
                    n = min(2, g1 - p0)
                    wb = B.wq_get()
                    deferred = None
                    for q in range(n):
                        pr = p0 + q
                        proj(sq_, PA, lambda k: wb[:, k, q * 256: q * 256 + 128], 8, lambda k, tb: hT[:, k, tb * TB:(tb + 1) * TB])
                        conv3(ug, PA, 2 * pr, L)
                        B.act(sg.all(), ug.all(), AF.Silu)
                        proj(sq_, PB, lambda k: wb[:, k, q * 256 + 128: q * 256 + 256], 8, lambda k, tb: hT[:, k, tb * TB:(tb + 1) * TB])

                        def fin_v(pr=pr):
                            conv3(uv, PB, 2 * pr + 1, L)
                            B.tt("pool", a[:, pr - g0, :], sg.all(), uv.all(), ALU.mult)
                        if cs_ is not None and q == n - 1:
                            deferred = fin_v
                        else:
                            fin_v()
                    if cs_ is not None:
                        for q in range(n):
                            pr = p0 + q
                            bi = pr % ncb
                            pgc = PA + (2 * q) * 512
                            pvc = PA + (2 * q + 1) * 512
                            for k in range(8):
                                B.mm(PS[:, pgc:pgc + Lc], wb[:, k, q * 256: q * 256 + 128], cs_.hT[:, k, :], k == 0, k == 7)
                            for k in range(8):
                                B.mm(PS[:, pvc:pvc + Lc], wb[:, k, q * 256 + 128: q * 256 + 256], cs_.hT[:, k, :], k == 0, k == 7)
                            conv3(ugc[bi], pgc, 2 * pr, Lc)
                            B.act(sgc[bi].all(), ugc[bi].all(), AF.Silu)
                            conv3(uvc[bi], pvc, 2 * pr + 1, Lc)
                            B.tt("pool", a_c[:, pr - g0, :], sgc[bi].all(), uvc[bi].all(), ALU.mult)
                        deferred()
                    if hook is not None:
                        B.ps_ptr = 3
                        hook(nblk)
                    nblk += 1
                    p0 += n
                nk = g1 - g0
                for cb in range(4):
                    wd = B.wq_get()
                    for h in range(2):
                        m = cb * 2 + h
                        pc = PA if h == 0 else PB
                        proj(sq_, pc, lambda k: wd[:, k, h * 128:(h + 1) * 128], nk, lambda k, tb: a[:, k, tb * TB:(tb + 1) * TB])
                        resid_add(sq_, m, pc, modv(i, "g2", m, col))
                    if cs_ is not None:
                        for h in range(2):
                            m = cb * 2 + h
                            pc = PA + h * 512
                            for k in range(nk):
                                B.mm(PS[:, pc:pc + Lc], wd[:, k, h * 128:(h + 1) * 128], a_c[:, k, :], k == 0, k == nk - 1)
                            resid_add(cs_, m, pc, modv(i, "g2", m, ccol))
            B.ps_ptr = 0

        R0 = B.const([128, 128], F32, "R0")
        Rq = B.const([128, 128], F32, "Rq")
        Rk = B.const([128, 128], F32, "Rk")
        qkg = B.const([128, 2], F32, "qkg")
        bdm = B.const([128, 2, 128], BF16, "bdm")
        bdf = B.scr([128, 2, 128], F32, 0, "bdf")
        pscl = B.const([128, 2], F32, "pscl")
        plc = B.const([128, 2, 17], F32, "plc")
        bones = B.const([128, 128], F32, "bones")
        etmp = B.const([128, 8], F32, "etmp")
        B.dma_in("sp", R0.all(), ropeR, "d0")
        B.dma_in("sp", qkg.all(), od_qkg, "d1")
        B.dma_in("sp", bdf.all(), od_bd, "d2")
        B.dma_in("sp", pscl.all(), od_ps, "d3")
        B.dma_in("sp", plc.all(), poolc, "d4")
        B.copy("dve", bdm.all(), bdf.all())
        B.ts("dve", Rq.all(), R0.all(), qkg[:, 0:1], ALU.mult)
        B.ts("dve", Rk.all(), R0.all(), qkg[:, 1:2], ALU.mult)
        B.memset("pool", bones.all(), 0.0)
        B.memset("pool", bones[0:64, 0:64], 1.0)
        B.memset("pool", bones[64:128, 64:128], 1.0)

        def odd_specs():
            B.wq_push(chunked(od_w_in, 1024, 1536), [128, 8, 512])
            B.wq_push(chunked(od_w_in, 0, 512), [128, 8, 512])
            B.wq_push(chunked(od_w_in, 512, 1024), [128, 8, 512])
            for cb in range(2):
                B.wq_push(chunked(od_w_out, cb * 512, (cb + 1) * 512), [128, 8, 512])

        def odd_mixer(col):
            L, TB, NB = S, 512, 4
            NKT = 18
            o = 0
            poolout = B.scr([128, 2, S], BF16, o, "poolout"); o += 2 * S * 2
            Vp = B.scr([128, NKT, 4, 128], BF16, o, "Vp"); o += NKT * 4 * 128 * 2
            o_after_v = o
            q_sb = B.scr([128, 6, S], BF16, o, "q_sb"); o += 6 * S * 2
            k_sb = B.scr([128, 2, S + LC], BF16, o, "k_sb"); o += 2 * (S + LC) * 2
            rtab = B.scr([128, 2, S], BF16, o, "rtab")
            rtab_off = o
            o += 2 * S * 2
            assert o <= SCR_BYTES, (o, SCR_BYTES)
            attn = B.sb([128, 8, S], BF16, OFF_HTX, "attn")
            o2 = o_after_v
            upad = B.scr([128, S + 16], F32, o2, "upad"); o2 += al((S + 16) * 4)
            bA = B.scr([128, S + 16], F32, o2, "bA"); o2 += al((S + 16) * 4)
            bB = B.scr([128, S + 16], F32, o2, "bB"); o2 += al((S + 16) * 4)
            pooled = B.scr([128, 2, S], BF16, o2, "pooled"); o2 += 2 * S * 2
            assert o2 <= SCR_BYTES, (o2, SCR_BYTES)
            tmpc = [B.sb([128, 512], F32, OFF_C + i_ * 2048, "tmpc") for i_ in range(4)]
            hx = lambda k, tb: HTX[:, k, tb * TB:(tb + 1) * TB]

            w2 = B.wq_get()
            B.memset("pool", Vp.all(), 1.0)
            B.memset("pool", upad[:, 0:8], 0.0)
            B.memset("pool", upad[:, S + 8:S + 16], 0.0)
            W = S + 16
            for ch in range(2):
                pc = B.psum(L)
                proj(SX, pc, lambda k: w2[:, k, 256 + ch * 128: 256 + (ch + 1) * 128], 8, hx)
                B.act(upad[:, 8:8 + S], PS[:, pc:pc + L], AF.Identity)
                B.tt("dve", bA[:, 1:W], upad[:, 0:W - 1], upad[:, 1:W], ALU.add)
                B.tt("dve", bB[:, 2:W - 1], bA[:, 1:W - 2], bA[:, 3:W], ALU.add)
                if ch == 0:
                    srcs = [(0, bA), (64, bB)]
                else:
                    B.tt("dve", bA[:, 4:W - 3], bB[:, 2:W - 5], bB[:, 6:W - 1], ALU.add)
                    B.tt("dve", bB[64:128, 8:W - 8], bA[64:128, 4:W - 12], bA[64:128, 12:W - 4], ALU.add)
                    srcs = [(0, bA), (64, bB)]
                for (p0, src) in srcs:
                    ps_ = slice(p0, p0 + 64)
                    B.stt("dve", pooled[ps_, ch, :], src[ps_, 8:8 + S], plc[ps_, ch, 0:1], upad[ps_, 8:8 + S],
                          ALU.mult, ALU.subtract)
                    for (c0, e0) in ((0, 1), (S - 8, 9)):
                        B.tt("dve", etmp[ps_, :], src[ps_, 8 + c0:16 + c0], plc[ps_, ch, e0:e0 + 8], ALU.mult)
                        B.tt("dve", pooled[ps_, ch, c0:c0 + 8], etmp[ps_, :], upad[ps_, 8 + c0:16 + c0], ALU.subtract)
            for kt in range(NKT):
                pc = B.psum(256)
                for k in range(8):
                    lh = HTC[:, k, kt * 128:(kt + 1) * 128] if kt < 2 else HTX[:, k, (kt - 2) * 128:(kt - 1) * 128]
                    B.mm(PS[:, pc:pc + 256], lh, w2[:, k, 0:256], k == 0, k == 7)
                for h in range(4):
                    bs = 64 * (h % 2)
                    if True:
                        B.act(Vp[:, kt, h, bs:bs + 64], PS[:, pc + h * 64:pc + (h + 1) * 64], AF.Identity)
                    else:
                        B.copy("dve", Vp[:, kt, h, bs:bs + 64], PS[:, pc + h * 64:pc + (h + 1) * 64])
            for ch in range(2):
                pc = B.psum(L)
                proj(SX, pc, lambda k: bdm[:, ch, :], 1, lambda k, tb: pooled[:, ch, tb * TB:(tb + 1) * TB])
                B.act(poolout[:, ch, :], PS[:, pc:pc + L], AF.Identity, scale=pscl[:, ch:ch + 1])
            B.ck("o2")
            B.dma_in("sp", rtab.all(), ropeT, "rtab")

            rope_n = [0]

            def rope(pv, n, gcol, Rm, outv, tok0, use_rope):
                par = rope_n[0] % 2
                rope_n[0] += 1
                qs = tmpc[2 * par][:, 0:n]
                qq = tmpc[2 * par + 1][:, 0:n]
                B.act(qs, pv, AF.Identity)
                B.act(qq, pv, AF.Square)
                pm = B.psum(n)
                rt_ = PS[:, pm:pm + n]
                B.mm(rt_, bones.all(), qq, True, True)
                if use_rope:
                    pq = B.psum(n)
                    B.mm(PS[:, pq:pq + n], Rm.all(), qs, True, True)
                B.act(rt_, rt_, AF.Ln, scale=1.0 / 64, bias=EPS)
                B.act(rt_, rt_, AF.Exp, scale=-0.5)
                if use_rope:
                    B.tt("dve", qq, PS[:, pq:pq + n], rtab[:, 1, tok0:tok0 + n], ALU.mult)
                    B.stt("dve", qs, qs, qkg[:, gcol:gcol + 1], rtab[:, 0, tok0:tok0 + n], ALU.mult, ALU.mult)
                    B.tt("pool", qs, qs, qq, ALU.add)
                    B.tt("dve", outv, qs, rt_, ALU.mult)
                else:
                    B.stt("dve", outv, qs, qkg[:, gcol:gcol + 1], rt_, ALU.mult, ALU.mult)

            w0 = B.wq_get()
            for j in range(4):
                pc = B.psum(L)
                proj(SX, pc, lambda k: w0[:, k, j * 128:(j + 1) * 128], 8, hx)
                for tb in range(NB):
                    rope(PS[:, pc + tb * TB:pc + (tb + 1) * TB], TB, 0, Rq, q_sb[:, j, tb * TB:(tb + 1) * TB], tb * TB, True)
            w1 = B.wq_get()
            for j in range(4, 6):
                pc = B.psum(L)
                proj(SX, pc, lambda k: w1[:, k, (j - 4) * 128:(j - 3) * 128], 8, hx)
                for tb in range(NB):
                    rope(PS[:, pc + tb * TB:pc + (tb + 1) * TB], TB, 0, Rq, q_sb[:, j, tb * TB:(tb + 1) * TB], tb * TB, True)
            for c_ in range(2):
                pc = B.psum(L)
                proj(SX, pc, lambda k: w1[:, k, 256 + c_ * 128:256 + (c_ + 1) * 128], 8, hx)
                for tb in range(NB):
                    rope(PS[:, pc + tb * TB:pc + (tb + 1) * TB], TB, 1, Rk,
                         k_sb[:, c_, LC + tb * TB:LC + (tb + 1) * TB], tb * TB, True)
                pc = B.psum(LC)
                for k in range(8):
                    B.mm(PS[:, pc:pc + LC], w1[:, k, 256 + c_ * 128:256 + (c_ + 1) * 128], HTC[:, k, :], k == 0, k == 7)
                rope(PS[:, pc:pc + LC], LC, 1, Rk, k_sb[:, c_, 0:LC], 0, False)
            B.ck("o3")
            ACC = 0
            STS = [1024, 2048, 3072]
            accS = [B.sb([128, 1024], F32, OFF_C, "accS0"), B.sb([128, 1024], F32, OFF_C + 4096, "accS1")]
            pT3 = [B.scr([128, 1024], BF16, rtab_off + i_ * 2048, "pT3") for i_ in range(3)]
            dsh = B.scr([128, 512], F32, rtab_off + 6144, "dsh")
            its = []
            for j in range(6):
                for qb in range(4):
                    for kt in range(NKT):
                        its.append((j, qb, kt))
            kvs = [[QORDER[2 * j + hb] // 3 for hb in range(2)] for j in range(6)]
            for j in range(6):
                assert kvs[j][0] % 2 == 0 and kvs[j][1] % 2 == 1

            def qk(i):
                j, qb, kt = its[i]
                st_ = STS[i % 3]
                for hb in range(2):
                    pr = slice(64 * hb, 64 * hb + 64)
                    B.mm(PS[:, st_ + hb * 512: st_ + (hb + 1) * 512], k_sb[pr, kvs[j][hb] // 2, kt * 128:(kt + 1) * 128],
                         q_sb[pr, j, qb * 512:(qb + 1) * 512], True, True)

            qk(0)
            qk(1)
            for i, (j, qb, kt) in enumerate(its):
                if i + 2 < len(its):
                    qk(i + 2)
                st_ = STS[i % 3]
                pT = pT3[i % 3]
                B.act(pT.all(), PS[:, st_:st_ + 1024], AF.Exp, scale=0.125)
                for hb in range(2):
                    B.mm(PS[:, ACC + hb * 512:ACC + (hb + 1) * 512], Vp[:, kt, kvs[j][hb], :], pT[:, hb * 512:(hb + 1) * 512],
                         kt == 0, kt == NKT - 1)
                if kt == NKT - 1:
                    g = i // NKT
                    aS = accS[g % 2]
                    B.copy("dve", aS.all(), PS[:, ACC:ACC + 1024])
                    B.P.add("sp", (lambda aS=aS: (lambda e: e.dma_start(out=dsh[0:64, :].ap, in_=aS[64:128, 0:512].ap)))(),
                            reads=[aS[64:128, 0:512]], writes=[dsh[0:64, :]], dma=True, semkey="dshA")
                    B.P.add("sp", (lambda aS=aS: (lambda e: e.dma_start(out=dsh[64:128, :].ap, in_=aS[0:64, 512:1024].ap)))(),
                            reads=[aS[0:64, 512:1024]], writes=[dsh[64:128, :]], dma=True, semkey="dshB")
                    B.recip(dsh.all(), dsh.all())
                    B.tt("dve", attn[0:64, j, qb * 512:(qb + 1) * 512], aS[0:64, 0:512], dsh[0:64, :], ALU.mult)
                    B.tt("dve", attn[64:128, j, qb * 512:(qb + 1) * 512], aS[64:128, 512:1024], dsh[64:128, :], ALU.mult)
            B.ps_ptr = 0
            for m in range(8):
                if m % 4 == 0:
                    w_ = B.wq_get()
                mm_ = m % 4
                pc = B.psum(L)
                proj(SX, pc, lambda k: w_[:, k, mm_ * 128:(mm_ + 1) * 128], 8,
                     lambda k, tb: (attn[:, k, tb * TB:(tb + 1) * TB] if k < 6 else poolout[:, k - 6, tb * TB:(tb + 1) * TB]))
                resid_add(SX, m, pc, modv(1, "g1", m, col))

        try:
            for b in range(BPC):
                for j in range(8):
                    B.dma_in("sp", X[:, j, :], xT[b, j * 128:(j + 1) * 128, :], "xin%d" % j)
                B.dma_in("sp", C.all(), ctxT[b].rearrange("(j p) t -> p j t", p=128), "cin")
                B.ck("load")
                even_specs(SX)
                even_specs(SC)
                ffn_specs(0, (lambda n: mod_spec_blk(1, n)) if b == 0 else None)
                odd_specs()
                ffn_specs(1)
                norm_mod(SX, 0, 0, b)
                if b == 0:
                    for cb_ in range(4, 12):
                        mod_blk(0, cb_)
                    mod_fin_n(0, 1)
                B.ck("norm")
                even_mixer(SX, b)
                B.ck("mixer")
                norm_mod(SX, 0, 1, b)
                B.ck("ffn")
                norm_mod(SC, 0, 0, 2)
                even_mixer(SC, 2)
                norm_mod(SC, 0, 1, 2)
                ffn(SX, 0, b, (lambda n: mod_blk(1, n)) if b == 0 else None, SC, 2)
                if b == 0:
                    mod_fin(1)
                B.ck("l0")
                norm_mod(SX, 1, 0, b)
                norm_mod(SC, 1, 0, 2)
                odd_mixer(b)
                B.ck("mixer1")
                norm_mod(SX, 1, 1, b)
                ffn(SX, 1, b)
                for j in range(8):
                    B.dma_out("sp", outT[b, j * 128:(j + 1) * 128, :], X[:, j, :], "xout%d" % j)
        except _Stop:
            for j in range(8):
                B.dma_out("sp", outT[0, j * 128:(j + 1) * 128, :], X[:, j, :], "xout%d" % j)
            if B.stop_after == "norm":
                hdbg = nc.dram_tensor("hdbg", [128, 8, S], BF16, kind="ExternalOutput").ap()
                B.dma_out("sp", hdbg, HTX.all(), "hdbg")
            B.wq_next = len(B.wq)

        assert B.wq_next == len(B.wq), (B.wq_next, len(B.wq))
        P.finalize(st)
        P.emit()
    return nc


def _fm(v, nchunk):
    v = np.asarray(v, np.float32)
    lead = v.shape[:-1]
    v = v.reshape(lead + (nchunk, 128))
    return np.ascontiguousarray(np.moveaxis(v, -1, 0))


def _consts():
    bf = ml_dtypes.bfloat16
    out = {}
    for name, L in (("dftL", S), ("dftC", LC)):
        l = np.arange(L, dtype=np.int64)
        ang = 2.0 * np.pi * ((l[:, None] * l[None, :]) % L).astype(np.float64) / L
        t = np.stack([np.cos(ang), -np.sin(ang)], axis=1) / math.sqrt(L)
        out[name] = t.astype(np.float32).astype(bf)
    c = np.arange(128, dtype=np.int64)
    ang = 2.0 * np.pi * ((c[:, None] * c[None, :]) % 128).astype(np.float64) / 128
    out["cs128"] = (np.concatenate([np.cos(ang), np.sin(ang)], axis=1) / math.sqrt(128)).astype(np.float32).astype(bf)
    out["identd"] = np.eye(128, dtype=np.float32)
    t = np.arange(S)
    freqs = (10000.0 ** (-np.arange(16, dtype=np.float32) / 16)).astype(np.float32)
    rows = (t // 64).astype(np.float32)
    cols = (t % 64).astype(np.float32)
    rt = np.zeros((128, 2, S), np.float32)
    R = np.zeros((128, 128), np.float32)
    for p in range(128):
        dd = p % 64
        blk = dd // 32
        i = dd % 16
        half = (dd % 32) // 16
        pos = rows if blk == 0 else cols
        ang_ = (pos * freqs[i]).astype(np.float32)
        rt[p, 0] = np.cos(ang_)
        rt[p, 1] = np.sin(ang_)
        partner = p + 16 if half == 0 else p - 16
        R[partner, p] = -1.0 if half == 0 else 1.0
    out["ropeT"] = rt.astype(bf)
    out["ropeR"] = R
    pc = np.zeros((128, 2, 17), np.float32)
    for ch in range(2):
        for hp in range(2):
            w = (2, 4, 8, 16)[ch * 2 + hp]
            sl = slice(hp * 64, hp * 64 + 64)
            pc[sl, ch, 0] = 1.0 / w
            for e in range(8):
                tt_ = e
                lo = max(tt_ - w // 2, 0)
                hi = min(tt_ + w // 2 - 1, S - 1) + 1
                pc[sl, ch, 1 + e] = 1.0 / (hi - lo)
                tt_ = S - 8 + e
                lo = max(tt_ - w // 2, 0)
                hi = min(tt_ + w // 2 - 1, S - 1) + 1
                pc[sl, ch, 9 + e] = 1.0 / (hi - lo)
    out["poolc"] = pc
    return out


_NC_CACHE = {}
QORDER = [0, 3, 1, 4, 2, 5, 6, 9, 7, 10, 8, 11]


def _prep(inputs):
    f = lambda k: np.asarray(inputs[k], np.float32)
    x, c, ctx, c_ctx = f("x"), f("c"), f("ctx"), f("c_ctx")
    shared = {}
    shared["w_ada"] = np.ascontiguousarray(f("w_ada"))
    shared["b_ada"] = np.ascontiguousarray(np.transpose(f("b_ada").reshape(2, 48, 128), (2, 0, 1)))
    ngv = np.stack([f("norm1_g"), f("norm2_g")], axis=1)
    shared["ng"] = np.ascontiguousarray(np.transpose(ngv.reshape(2, 2, 8, 128), (3, 0, 1, 2)))
    w_in = f("ev_w_in")[0]
    cols = list(range(512))
    for j in range(4):
        cols += list(range(1024 + j * 128, 1024 + (j + 1) * 128))
        cols += list(range(512 + j * 128, 512 + (j + 1) * 128))
    shared["ev_w_in"] = np.ascontiguousarray(w_in[:, cols])
    shared["ev_cw"] = np.ascontiguousarray(np.transpose(f("ev_conv_w")[0].reshape(31, 4, 128), (2, 1, 0)))
    shared["ev_ln"] = np.ascontiguousarray(np.transpose(
        np.stack([f("ev_ln_g")[0], f("ev_ln_b")[0]], 0).reshape(2, 4, 128), (2, 0, 1)))
    shared["ev_w_out"] = np.ascontiguousarray(f("ev_w_out")[0])
    ow = f("od_w_in")[0]
    qcols = []
    for h in QORDER:
        qcols += list(range(h * 64, (h + 1) * 64))
    shared["od_w_in"] = np.ascontiguousarray(np.concatenate([ow[:, qcols], ow[:, 768:]], axis=1))
    shared["od_qkg"] = np.ascontiguousarray(np.stack([np.tile(f("od_q_g")[0], 2), np.tile(f("od_k_g")[0], 2)], axis=1))
    pw = f("od_pool_w")[0]
    bd = np.zeros((128, 2, 128), np.float32)
    for ch in range(2):
        bd[0:64, ch, 0:64] = pw[2 * ch]
        bd[64:128, ch, 64:128] = pw[2 * ch + 1]
    shared["od_bd"] = bd
    shared["od_ps"] = np.ascontiguousarray(f("od_pool_scale")[0].reshape(2, 128).T)
    wo = f("od_w_out")[0]
    shared["od_w_out"] = np.ascontiguousarray(np.concatenate([wo[qcols, :], wo[768:, :]], axis=0))
    wu = f("ffn_w_up")
    fc = f("ffn_conv_w")
    ucols = []
    for pr in range(NPAIR):
        ucols += list(range(pr * 128, (pr + 1) * 128))
        ucols += list(range(DFF + pr * 128, DFF + (pr + 1) * 128))
    shared["w_up"] = np.ascontiguousarray(wu[:, :, ucols])
    fcp = fc[:, :, ucols]
    shared["fcw"] = np.ascontiguousarray(np.transpose(fcp.reshape(2, 3, 44, 128), (3, 0, 2, 1)))
    shared["w_dn"] = np.ascontiguousarray(f("ffn_w_down"))
    shared.update(_consts())
    maps = []
    for core in range(NCORES):
        bs = [core * BPC + i for i in range(BPC)]
        m = dict(shared)
        m["xT"] = np.ascontiguousarray(np.transpose(x[bs], (0, 2, 1)))
        m["ctxT"] = np.ascontiguousarray(np.transpose(ctx[bs], (0, 2, 1)))
        cc = np.stack([c[bs[0]], c[bs[1]], c_ctx], axis=1)
        m["cT"] = np.ascontiguousarray(np.transpose(cc.reshape(8, 128, 3), (1, 0, 2)))
        maps.append(m)
    return maps


def kernel(**inputs):
    maps = _prep(inputs)
    if "nc" not in _NC_CACHE:
        _NC_CACHE["nc"] = build()
    nc = _NC_CACHE["nc"]
    res = run_bass_kernel_spmd(nc, maps, core_ids=list(range(NCORES)))
    outs = []
    for core in range(NCORES):
        o = np.asarray(res.results[core]["outT"], np.float32)
        outs.append(np.transpose(o, (0, 2, 1)))
    return np.ascontiguousarray(np.concatenate(outs, axis=0))
```
